# Optimizing a Trainium2 kernel written in Bass

```python
import jax
import jax.numpy as jnp
from jax import lax
import numpy as np

D_MODEL = 4096
BATCH = 8
SEQ = 2048
DEPTH = 2

GRID_W = 64
CTX_LEN = 256
N_BRANCH = 4
BRANCH_W = D_MODEL // 4
HGRN_DK = 128
HGRN_DV = 128
HGRN_HEADS = BRANCH_W // HGRN_DV
HGRN_CHUNK = 64
HEAD_DIM = 128
ATTN_Q_HEADS = BRANCH_W // HEAD_DIM
ATTN_KV_HEADS = ATTN_Q_HEADS // 4
ROPE_AXIS_DIM = HEAD_DIM // 2
ROPE_THETA = 10000.0
Q_BLOCK = 128
SHORT_CONV_W = 3
CONF_CONV_W = 31
FFN_DIM = D_MODEL
FFN_RESIDUAL = 0.5
N_MOD = 9
EPS = 1e-6
LB_FLOOR = 1e-30

HK = HGRN_HEADS * HGRN_DK
HV = HGRN_HEADS * HGRN_DV
AQ = ATTN_Q_HEADS * HEAD_DIM
AKV = ATTN_KV_HEADS * HEAD_DIM
OFF_F_FWD = 0
OFF_F_BWD = OFF_F_FWD + HK
OFF_I = OFF_F_BWD + HK
OFF_K = OFF_I + HV
OFF_V = OFF_K + AKV
CTX_SIDE_COLS = OFF_V + AKV
OFF_HQ = CTX_SIDE_COLS
OFF_HG = OFF_HQ + HK
OFF_AQ = OFF_HG + HV
OFF_SC = OFF_AQ + AQ
OFF_GLU = OFF_SC + 3 * BRANCH_W
OFF_GATE = OFF_GLU + 2 * BRANCH_W
N_IN_COLS = OFF_GATE + N_BRANCH * D_MODEL

kernel_name = 'hybrid_diffusion_trunk'


def rms_norm(x, g):
    xf = x.astype(jnp.float32)
    y = xf * lax.rsqrt(jnp.mean(xf * xf, axis=-1, keepdims=True) + EPS)
    return (y * g.astype(jnp.float32)).astype(x.dtype)


def layer_norm(x, g, b):
    xf = x.astype(jnp.float32)
    xc = xf - jnp.mean(xf, axis=-1, keepdims=True)
    y = xc * lax.rsqrt(jnp.mean(xc * xc, axis=-1, keepdims=True) + EPS)
    return (y * g.astype(jnp.float32) + b.astype(jnp.float32)).astype(x.dtype)


def modulate(h, shift, scale):
    return h * (1 + scale) + shift


def depthwise_conv(u, w):
    width = w.shape[0]
    return lax.conv_general_dilated(u, w.astype(u.dtype)[:, None, :], window_strides=(1,),
                                    padding=[(width // 2, width // 2)],
                                    dimension_numbers=('NWC', 'WIO', 'NWC'),
                                    feature_group_count=u.shape[-1])


def swiglu(h, w1, w2):
    a, b = jnp.split(h @ w1, 2, axis=-1)
    return (jax.nn.silu(a) * b) @ w2


def half_ffn(s, shift, scale, gate, g_pre, g_post, w1, w2):
    y = swiglu(modulate(rms_norm(s, g_pre), shift, scale), w1, w2)
    return s + FFN_RESIDUAL * gate * rms_norm(y, g_post)


def to_heads(a, n_heads, d):
    return a.reshape(a.shape[0], a.shape[1], n_heads, d)


def flip_t(a):
    return jnp.flip(a, axis=1)


def grid_angles(n_tokens):
    rows = n_tokens // GRID_W
    r, col = jnp.meshgrid(jnp.arange(rows, dtype=jnp.float32), jnp.arange(GRID_W, dtype=jnp.float32), indexing='ij')
    inv = ROPE_THETA ** (-jnp.arange(0, ROPE_AXIS_DIM, 2, dtype=jnp.float32) / ROPE_AXIS_DIM)
    return r.reshape(-1, 1) * inv, col.reshape(-1, 1) * inv


def rope_axis(u, ang):
    u1, u2 = jnp.split(u, 2, axis=-1)
    cos = jnp.cos(ang)[None, :, None, :].astype(u.dtype)
    sin = jnp.sin(ang)[None, :, None, :].astype(u.dtype)
    return jnp.concatenate([u1 * cos - u2 * sin, u1 * sin + u2 * cos], axis=-1)


def rope_2d(u, ang_r, ang_c):
    return jnp.concatenate([rope_axis(u[..., :ROPE_AXIS_DIM], ang_r), rope_axis(u[..., ROPE_AXIS_DIM:], ang_c)], axis=-1)


def gqa_blocked(q, k, v):
    b, t, hq, d = q.shape
    hkv = k.shape[2]
    nb = t // Q_BLOCK
    qb = q.reshape(b, nb, Q_BLOCK, hkv, hq // hkv, d).transpose(1, 0, 2, 3, 4, 5)
    scale = d ** -0.5

    def one_block(qblk):
        s = jnp.einsum('bqhgd,bkhd->bhgqk', qblk, k).astype(jnp.float32) * scale
        p = jax.nn.softmax(s, axis=-1).astype(v.dtype)
        return jnp.einsum('bhgqk,bkhd->bqhgd', p, v)

    o = lax.map(one_block, qb)
    return o.transpose(1, 0, 2, 3, 4, 5).reshape(b, t, hq * d)


def hgrn_lower_bounds(lb_logits):
    p = jax.nn.softmax(lb_logits.astype(jnp.float32), axis=0)
    return jnp.cumsum(p, axis=0) - p[:1]


def hgrn_log_forget(a, lb):
    return jnp.logaddexp(jnp.log(jnp.maximum(lb, LB_FLOOR)), jnp.log1p(-lb) + jax.nn.log_sigmoid(a.astype(jnp.float32)))


def gla_chunked(q, log_f, v, s0):
    b, t, h, _ = q.shape
    dv = v.shape[-1]
    n = t // HGRN_CHUNK
    k = -jnp.expm1(log_f)

    def chunks(a):
        return a.reshape(b, n, HGRN_CHUNK, h, a.shape[-1]).transpose(1, 0, 3, 2, 4)

    lower = jnp.tril(jnp.ones((HGRN_CHUNK, HGRN_CHUNK), dtype=bool))[:, :, None]

    def step(s, blk):
        qb, kb, vb, fb = blk
        cum = jnp.cumsum(fb, axis=2)
        diff = cum[:, :, :, None, :] - cum[:, :, None, :, :]
        rel = jnp.where(lower, jnp.exp(jnp.minimum(diff, 0.0)), 0.0)
        scores = jnp.einsum('bhik,bhjk,bhijk->bhij', qb, kb, rel)
        out = jnp.einsum('bhij,bhjv->bhiv', scores, vb) + jnp.einsum('bhik,bhkv->bhiv', qb * jnp.exp(cum), s)
        last = cum[:, :, -1:, :]
        s = jnp.exp(last[:, :, 0, :, None]) * s + jnp.einsum('bhjk,bhjv->bhkv', kb * jnp.exp(last - cum), vb)
        return s, out

    s_fin, out = lax.scan(step, s0, (chunks(q), chunks(k), chunks(v), chunks(log_f)))
    return out.transpose(1, 0, 3, 2, 4).reshape(b, t, h, dv), s_fin


def gla_final_state(log_f, v):
    cum = jnp.cumsum(log_f, axis=1)
    k = -jnp.expm1(log_f)
    return jnp.einsum('bthk,bthv->bhkv', k * jnp.exp(cum[:, -1:] - cum), v)


def bidir_gla(q, lf_f, lf_b, v, s_f0, s_b0):
    o_f, s_f = gla_chunked(q, lf_f, v, s_f0)
    o_b, s_b = gla_chunked(flip_t(q), flip_t(lf_b), flip_t(v), s_b0)
    return o_f + flip_t(o_b), s_f, s_b


def hgrn_forget_value(z, lb_f, lb_b):
    lf_f = to_heads(hgrn_log_forget(z[..., OFF_F_FWD:OFF_F_BWD], lb_f), HGRN_HEADS, HGRN_DK)
    lf_b = to_heads(hgrn_log_forget(z[..., OFF_F_BWD:OFF_I], lb_b), HGRN_HEADS, HGRN_DK)
    v = to_heads(z[..., OFF_I:OFF_K].astype(jnp.float32), HGRN_HEADS, HGRN_DV)
    return lf_f, lf_b, v


def hgrn_query(z):
    return to_heads(z[..., OFF_HQ:OFF_HG].astype(jnp.float32), HGRN_HEADS, HGRN_DK)


def hgrn_readout(o, g_pre, norm_g):
    y = rms_norm(o, norm_g) * jax.nn.sigmoid(to_heads(g_pre.astype(jnp.float32), HGRN_HEADS, HGRN_DV))
    return y.reshape(o.shape[0], o.shape[1], HV).astype(g_pre.dtype)


def short_conv_branch(z, w):
    bg, cg, u = jnp.split(z[..., OFF_SC:OFF_GLU], 3, axis=-1)
    return bg * depthwise_conv(cg * u, w)


def conformer_conv_branch(z, dw_w, dw_b, ln_g, ln_b):
    a, gt = jnp.split(z[..., OFF_GLU:OFF_GATE], 2, axis=-1)
    u = depthwise_conv(a * jax.nn.sigmoid(gt), dw_w) + dw_b
    return jax.nn.silu(layer_norm(u, ln_g, ln_b))


def gated_merge(h, branches, w_in, w_branch, w_out):
    acc = None
    for i, br in enumerate(branches):
        lo = OFF_GATE + i * D_MODEL
        term = jax.nn.sigmoid(h @ w_in[:, lo:lo + D_MODEL]) * (br @ w_branch[i])
        acc = term if acc is None else acc + term
    return acc @ w_out


def token_mixers(h_lat, h_ctx, need_ctx, ang_r, ang_c, lb_f, lb_b, w_in, hgrn_norm_g, qk_norm_g,
                 short_conv_w, conf_dw_w, conf_dw_b, conf_ln_g, conf_ln_b, w_branch, w_out):
    z_lat = h_lat @ w_in[:, :OFF_GATE]
    z_ctx = h_ctx @ w_in[:, :(OFF_GATE if need_ctx else CTX_SIDE_COLS)]

    lf_cf, lf_cb, v_c = hgrn_forget_value(z_ctx, lb_f, lb_b)
    lf_lf, lf_lb, v_l = hgrn_forget_value(z_lat, lb_f, lb_b)
    if need_ctx:
        zero = jnp.zeros((h_ctx.shape[0], HGRN_HEADS, HGRN_DK, HGRN_DV), jnp.float32)
        o_c, s_f, s_b = bidir_gla(hgrn_query(z_ctx), lf_cf, lf_cb, v_c, zero, zero)
    else:
        s_f = gla_final_state(lf_cf, v_c)
        s_b = gla_final_state(flip_t(lf_cb), flip_t(v_c))
    o_l, _, _ = bidir_gla(hgrn_query(z_lat), lf_lf, lf_lb, v_l, s_f, s_b)
    hgrn_lat = hgrn_readout(o_l, z_lat[..., OFF_HG:OFF_AQ], hgrn_norm_g)

    k_c = rms_norm(to_heads(z_ctx[..., OFF_K:OFF_V], ATTN_KV_HEADS, HEAD_DIM), qk_norm_g[1])
    va_c = to_heads(z_ctx[..., OFF_V:CTX_SIDE_COLS], ATTN_KV_HEADS, HEAD_DIM)
    q_l = rope_2d(rms_norm(to_heads(z_lat[..., OFF_AQ:OFF_SC], ATTN_Q_HEADS, HEAD_DIM), qk_norm_g[0]), ang_r, ang_c)
    k_l = rope_2d(rms_norm(to_heads(z_lat[..., OFF_K:OFF_V], ATTN_KV_HEADS, HEAD_DIM), qk_norm_g[1]), ang_r, ang_c)
    va_l = to_heads(z_lat[..., OFF_V:CTX_SIDE_COLS], ATTN_KV_HEADS, HEAD_DIM)
    att_lat = gqa_blocked(q_l, jnp.concatenate([k_c, k_l], axis=1), jnp.concatenate([va_c, va_l], axis=1))

    y_lat = gated_merge(h_lat, [hgrn_lat, att_lat, short_conv_branch(z_lat, short_conv_w),
                                conformer_conv_branch(z_lat, conf_dw_w, conf_dw_b, conf_ln_g, conf_ln_b)],
                        w_in, w_branch, w_out)
    if not need_ctx:
        return y_lat, None

    hgrn_ctx = hgrn_readout(o_c, z_ctx[..., OFF_HG:OFF_AQ], hgrn_norm_g)
    q_c = rms_norm(to_heads(z_ctx[..., OFF_AQ:OFF_SC], ATTN_Q_HEADS, HEAD_DIM), qk_norm_g[0])
    att_ctx = gqa_blocked(q_c, k_c, va_c)
    y_ctx = gated_merge(h_ctx, [hgrn_ctx, att_ctx, short_conv_branch(z_ctx, short_conv_w),
                                conformer_conv_branch(z_ctx, conf_dw_w, conf_dw_b, conf_ln_g, conf_ln_b)],
                        w_in, w_branch, w_out)
    return y_lat, y_ctx


def setup_inputs(seed: int = 0) -> dict:
    key = jax.random.key(seed)
    ks = jax.random.split(key, 20)

    def nrm(k, shape, scale):
        return jax.random.normal(k, shape, jnp.float32) * scale

    def gain(k, shape):
        return 1.0 + nrm(k, shape, 0.02)

    return {
        'x': nrm(ks[0], (BATCH, SEQ, D_MODEL), 1.0),
        'c': nrm(ks[1], (BATCH, D_MODEL), 1.0),
        'ctx': nrm(ks[2], (BATCH, CTX_LEN, D_MODEL), 1.0),
        'c_ctx': nrm(ks[3], (D_MODEL,), 1.0),
        'w_mod': nrm(ks[4], (DEPTH, D_MODEL, N_MOD * D_MODEL), 0.5 * D_MODEL ** -0.5),
        'b_mod': nrm(ks[5], (DEPTH, N_MOD * D_MODEL), 0.02),
        'norm_g': gain(ks[6], (DEPTH, 6, D_MODEL)),
        'ffn_w1': nrm(ks[7], (DEPTH, 2, D_MODEL, 2 * FFN_DIM), D_MODEL ** -0.5),
        'ffn_w2': nrm(ks[8], (DEPTH, 2, FFN_DIM, D_MODEL), FFN_DIM ** -0.5),
        'w_in': nrm(ks[9], (DEPTH, D_MODEL, N_IN_COLS), D_MODEL ** -0.5),
        'hgrn_lb_logits': nrm(ks[10], (DEPTH, 2, HK), 1.0),
        'hgrn_norm_g': gain(ks[11], (DEPTH, HGRN_DV)),
        'qk_norm_g': gain(ks[12], (DEPTH, 2, HEAD_DIM)),
        'short_conv_w': nrm(ks[13], (DEPTH, SHORT_CONV_W, BRANCH_W), SHORT_CONV_W ** -0.5),
        'conf_dw_w': nrm(ks[14], (DEPTH, CONF_CONV_W, BRANCH_W), CONF_CONV_W ** -0.5),
        'conf_dw_b': nrm(ks[15], (DEPTH, BRANCH_W), 0.02),
        'conf_ln_g': gain(ks[16], (DEPTH, BRANCH_W)),
        'conf_ln_b': nrm(ks[17], (DEPTH, BRANCH_W), 0.02),
        'w_branch': nrm(ks[18], (DEPTH, N_BRANCH, BRANCH_W, D_MODEL), BRANCH_W ** -0.5),
        'w_out': nrm(ks[19], (DEPTH, D_MODEL, D_MODEL), D_MODEL ** -0.5),
    }


def reference(x, c, ctx, c_ctx, w_mod, b_mod, norm_g, ffn_w1, ffn_w2, w_in, hgrn_lb_logits, hgrn_norm_g,
              qk_norm_g, short_conv_w, conf_dw_w, conf_dw_b, conf_ln_g, conf_ln_b, w_branch, w_out):
    ang_r, ang_c = grid_angles(x.shape[1])
    lbs = hgrn_lower_bounds(hgrn_lb_logits)
    for l in range(DEPTH):
        need_ctx = l < DEPTH - 1
        m_lat = [m[:, None, :] for m in jnp.split(jax.nn.silu(c) @ w_mod[l] + b_mod[l], N_MOD, axis=-1)]
        m_ctx = jnp.split(jax.nn.silu(c_ctx) @ w_mod[l] + b_mod[l], N_MOD, axis=-1)
        g = norm_g[l]
        x = half_ffn(x, m_lat[0], m_lat[1], m_lat[2], g[0], g[1], ffn_w1[l, 0], ffn_w2[l, 0])
        ctx = half_ffn(ctx, m_ctx[0], m_ctx[1], m_ctx[2], g[0], g[1], ffn_w1[l, 0], ffn_w2[l, 0])
        y_lat, y_ctx = token_mixers(modulate(rms_norm(x, g[2]), m_lat[3], m_lat[4]),
                                    modulate(rms_norm(ctx, g[2]), m_ctx[3], m_ctx[4]),
                                    need_ctx, ang_r, ang_c, lbs[l, 0], lbs[l, 1], w_in[l], hgrn_norm_g[l],
                                    qk_norm_g[l], short_conv_w[l], conf_dw_w[l], conf_dw_b[l], conf_ln_g[l],
                                    conf_ln_b[l], w_branch[l], w_out[l])
        x = x + m_lat[5] * rms_norm(y_lat, g[3])
        x = half_ffn(x, m_lat[6], m_lat[7], m_lat[8], g[4], g[5], ffn_w1[l, 1], ffn_w2[l, 1])
        if need_ctx:
            ctx = ctx + m_ctx[5] * rms_norm(y_ctx, g[3])
            ctx = half_ffn(ctx, m_ctx[6], m_ctx[7], m_ctx[8], g[4], g[5], ffn_w1[l, 1], ffn_w2[l, 1])
    return x
```

```python
import numpy as np
import concourse.bass as bass
import concourse.mybir as mybir
from concourse.bass_utils import run_bass_kernel_spmd

F32 = mybir.dt.float32
BF16 = mybir.dt.bfloat16
AF = mybir.ActivationFunctionType
ALU = mybir.AluOpType

D = 4096
TL = 2048
TC = 256
T = TL + TC
KC = 32
EPS = 1e-6
NIN = 28160
HALVES = [(0, 1024, [(0, 512), (512, 512)]), (1024, 1280, [(1024, 512), (1536, 512), (2048, 256)])]
HSEG = [[(0, 0, 1024)], [(0, 1024, 2048), (1, 2048, 2304)]]


class Buf:
    __slots__ = ("w", "r")

    def __init__(self):
        self.w = None
        self.r = []


class KB:
    NDMA = 24

    def __init__(self):
        self.nc = bass.Bass("TRN2", target_bir_lowering=False)
        nc = self.nc
        self.eng = {"pe": nc.tensor, "act": nc.scalar, "dve": nc.vector, "pool": nc.gpsimd, "sp": nc.sync}
        self.sem = {k: nc.alloc_semaphore("s_" + k) for k in ("pe", "act", "dve", "pool")}
        self.cnt = {k: 0 for k in self.sem}
        self.dsem = [nc.alloc_semaphore("d%d" % i) for i in range(self.NDMA)]
        self.dcnt = [0] * self.NDMA
        self.dnext = 0
        self.seen = {k: {} for k in self.eng}
        self.ARENA = 51712 * 4
        self.arena = nc.alloc_sbuf_tensor("arena", [128, self.ARENA // 4], F32)
        self.sb_off = 0
        self.ps = nc.alloc_psum_tensor("psall", [128, 8, 512], F32)
        self.bps = [Buf() for _ in range(8)]
        self.pnext = 0

    def sb(self, shape, dtype):
        nb = int(np.prod(shape[1:])) * (2 if dtype == BF16 else 4)
        nb = (nb + 63) // 64 * 64
        o = self.sb_off
        self.sb_off += nb
        assert self.sb_off <= self.ARENA, ("sbuf overflow", self.sb_off)
        v = self.arena[:, o // 4:(o + nb) // 4]
        if dtype == BF16:
            v = v.bitcast(BF16)
        n = int(np.prod(shape[1:]))
        v = v[0:shape[0], 0:n]
        if len(shape) == 3:
            v = v.rearrange("p (a b) -> p a b", b=shape[2])
        elif len(shape) == 4:
            v = v.rearrange("p (a b c) -> p a b c", b=shape[2], c=shape[3])
        return v

    def banks(self, n):
        if self.pnext + n > 8:
            self.pnext = 0
        b = list(range(self.pnext, self.pnext + n))
        self.pnext = (self.pnext + n) % 8
        return b

    def pflat(self, banks, ncols):
        return self.ps[:, banks[0]:banks[0] + len(banks), :].rearrange("p a b -> p (a b)")[:, 0:ncols]

    def _wait(self, e, tickets):
        eng = self.eng[e]
        seen = self.seen[e]
        best = {}
        for t in tickets:
            if t is None:
                continue
            s, v, own = t
            if own == "pe" and e == "pe":
                continue
            if seen.get(id(s), 0) >= v:
                continue
            if best.get(id(s), (None, 0))[1] < v:
                best[id(s)] = (s, v)
        for s, v in best.values():
            eng.wait_ge(s, v)
            seen[id(s)] = v

    @staticmethod
    def _deps(reads, writes):
        need = []
        for b in reads:
            need.append(b.w)
        for b in writes:
            need.append(b.w)
            need.extend(b.r)
        return need

    @staticmethod
    def _mark(t, reads, writes):
        for b in reads:
            b.r.append(t)
            if len(b.r) > 48:
                b.r = b.r[-48:]
        for b in writes:
            b.w = t
            b.r = []

    def op(self, e, fn, reads=(), writes=()):
        self._wait(e, self._deps(reads, writes))
        ins = fn(self.eng[e])
        self.cnt[e] += 1
        ins.then_inc(self.sem[e], 1)
        t = (self.sem[e], self.cnt[e], e)
        self._mark(t, reads, writes)
        return t

    def dma(self, q, out, in_, reads=(), writes=()):
        i = self.dnext
        self.dnext = (self.dnext + 1) % self.NDMA
        s = self.dsem[i]
        need = self._deps(reads, writes)
        if self.dcnt[i] > 0:
            need.append((s, self.dcnt[i], "dma"))
        self._wait(q, need)
        self.eng[q].dma_start(out=out, in_=in_).then_inc(s, 16)
        self.dcnt[i] += 16
        t = (s, self.dcnt[i], "dma")
        self._mark(t, reads, writes)
        return t

    def barrier(self):
        alls = [(self.sem[k], self.cnt[k], k + "_b") for k in self.sem if self.cnt[k] > 0]
        alls += [(self.dsem[i], self.dcnt[i], "dma") for i in range(self.NDMA) if self.dcnt[i] > 0]
        for e in self.eng:
            self._wait(e, alls)
        for b in self.bps:
            b.w = None
            b.r = []


def build(stop_after=None, dbg=()):
    k = KB()
    nc = k.nc
    ps = k.ps
    bps = k.bps

    def din(name, shape, dt=F32):
        return nc.dram_tensor(name, list(shape), dt, kind="ExternalInput").ap()

    def dscr(name, shape, dt=F32):
        kind = "ExternalOutput" if name in dbg else "Internal"
        return nc.dram_tensor(name, list(shape), dt, kind=kind).ap()

    x_in = din("x", [TL, D])
    ctx_in = din("ctx", [TC, D])
    cT_in = din("cT", [128, KC, 2])
    w_mod = din("w_mod", [2, D, 9 * D])
    bmodT = din("bmodT", [128, 2, 288])
    ngT = din("ngT", [128, 2, 6, 32])
    ffn_w1 = din("ffn_w1", [2, 2, D, 2 * D])
    ffn_w2 = din("ffn_w2", [2, 2, D, D])
    w_in = din("w_in", [2, D, NIN])
    lblT = din("lblT", [128, 2, 16])
    hgnT = din("hgnT", [128, 2])
    qknT = din("qknT", [128, 4])
    scwT = din("scwT", [128, 2, 8, 3])
    cfwT = din("cfwT", [128, 2, 8, 31])
    cfbT = din("cfbT", [128, 2, 8])
    lngT = din("lngT", [128, 2, 8])
    lnbT = din("lnbT", [128, 2, 8])
    w_branch = din("w_branch", [2, 4, 1024, D])
    w_out = din("w_out", [2, D, D])
    idn_in = din("idn", [128, 128])
    rot_in = din("rot", [128, 128])
    cos_in = din("cosT", [128, TL])
    sin_in = din("sinT", [128, TL])
    chm_in = din("chm", [128, T])
    tri_in = din("tri", [32, 2, 512])
    out_d = nc.dram_tensor("out", [TL, D], F32, kind="ExternalOutput").ap()

    sT = dscr("sT", [D, T])
    yT = dscr("yT", [D, T])
    uT = dscr("uT", [D, T], BF16)
    zF = dscr("zF", [2048, T])
    zHQ = dscr("zHQ", [1024, T])
    zHG = dscr("zHG", [1024, T])
    zAQ = dscr("zAQ", [1024, T])
    zK = dscr("zK", [256, T])
    zSC = dscr("zSC", [3072, T])
    zGLU = dscr("zGLU", [1024, T])
    zGATE = dscr("zGATE", [4 * D, T])
    vI = dscr("vI", [T, 1024], BF16)
    vV = dscr("vV", [T, 256], BF16)
    brT = dscr("brT", [D, T], BF16)
    accT = dscr("accT", [D, T], BF16)
    oDbg = dscr("oDbg", [1024, T])

    def rows(ap, c):
        return ap[c * 128:(c + 1) * 128, :]

    ONES_F = k.sb([128, 128], F32)
    ONES_B = k.sb([128, 128], BF16)
    IDN = k.sb([128, 128], F32)
    RS = k.sb([128, T], F32)
    ABC = k.sb([128, 18, 32, 2], F32)
    LB = k.sb([128, 2, 16], F32)
    OMLB = k.sb([128, 2, 16], F32)
    HGN = k.sb([128, 2], F32)
    QKN = k.sb([128, 4], F32)
    bONES, bIDN, bRS, bABC, bSM = Buf(), Buf(), Buf(), Buf(), Buf()
    PBASE = k.sb_off

    def abc(l, u, w):
        return ABC[:, (l * 3 + u) * 3 + w, :, :]

    k.op("dve", lambda e: e.memset(ONES_F[:, :], 1.0), writes=[bONES])
    k.op("dve", lambda e: e.memset(ONES_B[:, :], 1.0), writes=[bONES])
    k.dma("sp", IDN[:, :], idn_in, writes=[bIDN])
    k.dma("sp", HGN[:, :], hgnT, writes=[bSM])
    k.dma("sp", QKN[:, :], qknT, writes=[bSM])

    def phase_end():
        k.barrier()
        k.sb_off = PBASE

    def rs_from_psum(banks5, scale):
        src = k.pflat(banks5, T)
        k.op("act", lambda e: e.activation(out=RS[:, :], in_=src, func=AF.Sqrt, scale=scale, bias=EPS),
             reads=[bps[b] for b in banks5], writes=[bRS])
        k.op("dve", lambda e: e.reciprocal(out=RS[:, :], in_=RS[:, :]), reads=[bRS], writes=[bRS])

    def ones_acc(src_ap, bsrc, col0, first, last, lhs=None):
        hi = 0 if col0 == 0 else 1
        _, ncols, tiles = HALVES[hi]

        def mm(e):
            ins = None
            for (t0, tn) in tiles:
                ins = e.matmul(ps[:, t0 // 512, 0:tn], lhsT=(ONES_F if lhs is None else lhs)[:, :],
                               rhs=src_ap[:, t0 - col0:t0 - col0 + tn], start=first, stop=last)
            return ins
        k.op("pe", mm, reads=[bsrc, bONES], writes=[bps[t0 // 512] for (t0, tn) in tiles])

    def p_transpose_in():
        XT = [k.sb([128, D], F32) for _ in range(2)]
        bXT = [Buf(), Buf()]
        ST = [k.sb([128, KC, 128], F32) for _ in range(2)]
        bST = [Buf(), Buf()]
        sTv = sT.rearrange("(c p) t -> p c t", p=128)
        for tt in range(18):
            src = x_in[tt * 128:(tt + 1) * 128, :] if tt < 16 else ctx_in[(tt - 16) * 128:(tt - 15) * 128, :]
            X, bX = XT[tt % 2], bXT[tt % 2]
            S, bS = ST[tt % 2], bST[tt % 2]
            k.dma("sp", X[:, :], src, writes=[bX])
            for c4 in range(8):
                b = k.banks(1)[0]

                def tr(e):
                    ins = None
                    for q in range(4):
                        c = c4 * 4 + q
                        ins = e.transpose(ps[:, b, q * 128:(q + 1) * 128], X[:, c * 128:(c + 1) * 128], IDN[:, :])
                    return ins
                k.op("pe", tr, reads=[bX, bIDN], writes=[bps[b]])
                eng = "act" if c4 % 2 == 0 else "dve"
                dst = S[:, c4 * 4:(c4 + 1) * 4, :].rearrange("p a b -> p (a b)")
                if eng == "act":
                    k.op("act", lambda e: e.copy(out=dst, in_=ps[:, b, :]), reads=[bps[b]], writes=[bS])
                else:
                    k.op("dve", lambda e: e.tensor_copy(out=dst, in_=ps[:, b, :]), reads=[bps[b]], writes=[bS])
            k.dma("sp", sTv[:, :, tt * 128:(tt + 1) * 128], S[:, :, :], reads=[bS])
        phase_end()

    def p_stats():
        SG = [k.sb([128, 1280], F32) for _ in range(3)]
        bSG = [Buf() for _ in range(3)]
        i = 0
        for c in range(KC):
            for hi, (c0, ncol, tiles) in enumerate(HALVES):
                S, bS = SG[i % 3], bSG[i % 3]
                i += 1
                k.dma("sp", S[:, 0:ncol], rows(sT, c)[:, c0:c0 + ncol], writes=[bS])
                k.op("act", lambda e: e.activation(out=S[:, 0:ncol], in_=S[:, 0:ncol], func=AF.Square), reads=[bS], writes=[bS])
                ones_acc(S, bS, c0, c == 0, c == KC - 1)
        rs_from_psum([0, 1, 2, 3, 4], 1.0 / D)
        phase_end()

    def p_mod():
        CT = k.sb([128, KC, 2], F32)
        SIL = k.sb([128, KC, 2], BF16)
        MODT = k.sb([128, 2, 288, 2], F32)
        BM = k.sb([128, 2, 288], F32)
        NG = k.sb([128, 2, 6, 32], F32)
        LBL = k.sb([128, 2, 16], F32)
        MR = [k.sb([2, 512], F32) for _ in range(2)]
        WM = [k.sb([128, KC, 512], BF16) for _ in range(3)]
        bCT, bSIL, bMODT, bBM, bNG, bLBL = Buf(), Buf(), Buf(), Buf(), Buf(), Buf()
        bMR = [Buf(), Buf()]
        bWM = [Buf() for _ in range(3)]
        k.dma("sp", CT[:, :, :], cT_in, writes=[bCT])
        k.dma("sp", BM[:, :, :], bmodT, writes=[bBM])
        k.dma("sp", NG[:, :, :, :], ngT, writes=[bNG])
        k.dma("sp", LBL[:, :, :], lblT, writes=[bLBL])
        k.op("act", lambda e: e.activation(out=SIL[:, :, :], in_=CT[:, :, :], func=AF.Silu), reads=[bCT], writes=[bSIL])
        k.op("dve", lambda e: e.memset(LB[:, 0, :], 0.0), writes=[bSM])
        k.op("dve", lambda e: e.tensor_tensor(out=LB[:, 1, :], in0=LBL[:, 1, :], in1=LBL[:, 0, :], op=ALU.subtract), reads=[bLBL], writes=[bSM])
        k.op("act", lambda e: e.activation(out=LB[:, 1, :], in_=LB[:, 1, :], func=AF.Exp, scale=-1.0), reads=[bSM], writes=[bSM])
        k.op("dve", lambda e: e.tensor_scalar(out=LB[:, 1, :], in0=LB[:, 1, :], scalar1=1.0, scalar2=None, op0=ALU.add), reads=[bSM], writes=[bSM])
        k.op("dve", lambda e: e.reciprocal(out=LB[:, 1, :], in_=LB[:, 1, :]), reads=[bSM], writes=[bSM])
        k.op("dve", lambda e: e.tensor_scalar(out=OMLB[:, :, :], in0=LB[:, :, :], scalar1=-1.0, scalar2=1.0, op0=ALU.mult, op1=ALU.add), reads=[bSM], writes=[bSM])
        nblk = 72
        blocks = [(l, j) for l in range(2) for j in range(nblk)]

        def load(i):
            l, j = blocks[i]
            k.dma("pool", WM[i % 3][:, :, :], w_mod[l].rearrange("(kc p) n -> p kc n", p=128)[:, :, j * 512:(j + 1) * 512], writes=[bWM[i % 3]])
        load(0)
        load(1)
        for i, (l, j) in enumerate(blocks):
            if i + 2 < len(blocks):
                load(i + 2)
            W, bW = WM[i % 3], bWM[i % 3]
            b = k.banks(1)[0]

            def mm(e):
                ins = None
                for kc in range(KC):
                    ins = e.matmul(ps[0:2, b, :], lhsT=SIL[:, kc, :], rhs=W[:, kc, :], start=(kc == 0), stop=(kc == KC - 1))
                return ins
            k.op("pe", mm, reads=[bSIL, bW], writes=[bps[b]])
            M, bM = MR[i % 2], bMR[i % 2]
            k.op("act", lambda e: e.copy(out=M[:, :], in_=ps[0:2, b, :]), reads=[bps[b]], writes=[bM])
            b2 = k.banks(1)[0]

            def tr(e):
                ins = None
                for q in range(4):
                    ins = e.matmul(ps[:, b2, q * 2:(q + 1) * 2], lhsT=M[:, q * 128:(q + 1) * 128], rhs=IDN[0:2, 0:2], start=True, stop=True)
                return ins
            k.op("pe", tr, reads=[bM, bIDN], writes=[bps[b2]])
            k.op("dve", lambda e: e.tensor_copy(out=MODT[:, l, j * 4:(j + 1) * 4, :].rearrange("p a b -> p (a b)"), in_=ps[:, b2, 0:8]),
                 reads=[bps[b2]], writes=[bMODT])
        for s in range(2):
            k.op("dve", lambda e: e.tensor_tensor(out=MODT[:, :, :, s], in0=MODT[:, :, :, s], in1=BM[:, :, :], op=ALU.add), reads=[bMODT, bBM], writes=[bMODT])
        for l in range(2):
            for u in range(3):
                for s in range(2):
                    sh = MODT[:, l, (3 * u) * 32:(3 * u + 1) * 32, s]
                    sc = MODT[:, l, (3 * u + 1) * 32:(3 * u + 2) * 32, s]
                    gt = MODT[:, l, (3 * u + 2) * 32:(3 * u + 3) * 32, s]
                    A, B, C = abc(l, u, 0)[:, :, s], abc(l, u, 1)[:, :, s], abc(l, u, 2)[:, :, s]
                    k.op("dve", lambda e: e.scalar_tensor_tensor(out=A, in0=sc, scalar=1.0, in1=NG[:, l, 2 * u, :], op0=ALU.add, op1=ALU.mult), reads=[bMODT, bNG], writes=[bABC])
                    k.op("dve", lambda e: e.tensor_copy(out=B, in_=sh), reads=[bMODT], writes=[bABC])
                    mult = 1.0 if u == 1 else 0.5
                    k.op("dve", lambda e: e.scalar_tensor_tensor(out=C, in0=gt, scalar=mult, in1=NG[:, l, 2 * u + 1, :], op0=ALU.mult, op1=ALU.mult), reads=[bMODT, bNG], writes=[bABC])
        phase_end()

    class G:
        pass

    def gemm_setup(nstage=4, nbf=2):
        g = G()
        g.HT = k.sb([128, KC, T], BF16)
        g.bHT = [Buf() for _ in range(KC)]
        g.WB = [k.sb([128, KC, 128], BF16) for _ in range(2)]
        g.bWB = [Buf(), Buf()]
        g.ST = [k.sb([128, 1280], F32) for _ in range(nstage)]
        g.bST = [Buf() for _ in range(nstage)]
        g.SB = [k.sb([128, 1280], BF16) for _ in range(nbf)]
        g.bSB = [Buf() for _ in range(nbf)]
        g.si = 0
        g.bi = 0
        return g

    def stg(g):
        i = g.si % len(g.ST)
        g.si += 1
        return g.ST[i], g.bST[i]

    def stgb(g):
        i = g.bi % len(g.SB)
        g.bi += 1
        return g.SB[i], g.bSB[i]

    def prologue_norm(g, l, u):
        for c in range(KC):
            for hi, (c0, ncol, tiles) in enumerate(HALVES):
                S, bS = stg(g)
                k.dma("sp", S[:, 0:ncol], rows(sT, c)[:, c0:c0 + ncol], writes=[bS])
                k.op("dve", lambda e: e.tensor_tensor(out=S[:, 0:ncol], in0=S[:, 0:ncol], in1=RS[:, c0:c0 + ncol], op=ALU.mult), reads=[bS, bRS], writes=[bS])
                for (seg, lo, hi2) in HSEG[hi]:
                    k.op("act", lambda e: e.activation(out=g.HT[:, c, lo:hi2], in_=S[:, lo - c0:hi2 - c0], func=AF.Identity,
                                                       scale=abc(l, u, 0)[:, c, seg:seg + 1], bias=abc(l, u, 1)[:, c, seg:seg + 1]),
                         reads=[bS, bABC], writes=[g.bHT[c]])

    def prologue_load(g, src, nchunks=KC):
        for c in range(nchunks):
            k.dma("sp", g.HT[:, c, :], rows(src, c), writes=[g.bHT[c]])

    def gemm(g, chunks, wsrc, kc0_of, nkc, epi):
        n = len(chunks)

        def load(i):
            k.dma("pool", g.WB[i % 2][:, 0:nkc, :], wsrc(chunks[i]), writes=[g.bWB[i % 2]])
        load(0)
        for i, desc in enumerate(chunks):
            if i + 1 < n:
                load(i + 1)
            w, bw = g.WB[i % 2], g.bWB[i % 2]
            kc0 = kc0_of(desc)
            for hi, (c0, ncol, tiles) in enumerate(HALVES):
                banks = k.banks(len(tiles))

                def mm(e):
                    ins = None
                    for kc in range(nkc):
                        for bi, (t0, tn) in zip(banks, tiles):
                            ins = e.matmul(ps[:, bi, 0:tn], lhsT=w[:, kc, :], rhs=g.HT[:, kc0 + kc, t0:t0 + tn],
                                           start=(kc == 0), stop=(kc == nkc - 1))
                    return ins
                k.op("pe", mm, reads=[bw] + g.bHT[kc0:kc0 + nkc], writes=[bps[b] for b in banks])
                epi(desc, hi, banks)

    def wview(w2d, col0, nkc=KC, row0=0):
        return w2d[row0:row0 + nkc * 128, col0:col0 + 128].rearrange("(kc p) n -> p kc n", p=128)

    def epi_store(g, dst, func=None):
        def f(desc, hi, banks):
            j = desc[1]
            c0, ncol, _ = HALVES[hi]
            S, bS = stg(g)
            src = k.pflat(banks, ncol)
            if func is None:
                k.op("act", lambda e: e.copy(out=S[:, 0:ncol], in_=src), reads=[bps[b] for b in banks], writes=[bS])
            else:
                k.op("act", lambda e: e.activation(out=S[:, 0:ncol], in_=src, func=func), reads=[bps[b] for b in banks], writes=[bS])
            k.dma("sp", rows(dst, j)[:, c0:c0 + ncol], S[:, 0:ncol], reads=[bS])
        return f

    def epi_pair(g, dst, func, hold):
        def f(desc, hi, banks):
            kind, j = desc[0], desc[1]
            c0, ncol, _ = HALVES[hi]
            src = k.pflat(banks, ncol)
            if kind == "g":
                S, bS = stg(g)
                hold[hi] = (S, bS)
                k.op("act", lambda e: e.activation(out=S[:, 0:ncol], in_=src, func=func), reads=[bps[b] for b in banks], writes=[bS])
            else:
                S, bS = hold[hi]
                if dst.dtype == BF16:
                    U, bU = stgb(g)
                else:
                    U, bU = stg(g)
                k.op("dve", lambda e: e.tensor_tensor(out=U[:, 0:ncol], in0=S[:, 0:ncol], in1=src, op=ALU.mult),
                     reads=[bS] + [bps[b] for b in banks], writes=[bU])
                k.dma("sp", rows(dst, j)[:, c0:c0 + ncol], U[:, 0:ncol], reads=[bU])
        return f

    def epi_y(g):
        def f(desc, hi, banks):
            j = desc[1]
            c0, ncol, _ = HALVES[hi]
            src = k.pflat(banks, ncol)
            S, bS = stg(g)
            k.op("act", lambda e: e.copy(out=S[:, 0:ncol], in_=src), reads=[bps[b] for b in banks], writes=[bS])
            k.dma("sp", rows(yT, j)[:, c0:c0 + ncol], S[:, 0:ncol], reads=[bS])
            if desc[2]:
                k.op("act", lambda e: e.activation(out=RS[:, c0:c0 + ncol], in_=src, func=AF.Square), reads=[bps[b] for b in banks], writes=[bRS])
            else:
                Q, bQ = stg(g)
                k.op("act", lambda e: e.activation(out=Q[:, 0:ncol], in_=src, func=AF.Square), reads=[bps[b] for b in banks], writes=[bQ])
                k.op("dve", lambda e: e.tensor_tensor(out=RS[:, c0:c0 + ncol], in0=RS[:, c0:c0 + ncol], in1=Q[:, 0:ncol], op=ALU.add), reads=[bQ, bRS], writes=[bRS])
        return f

    def rs_finalize(scale):
        for hi, (c0, ncol, tiles) in enumerate(HALVES):
            ones_acc(RS[:, c0:c0 + ncol], bRS, c0, True, True)
        rs_from_psum([0, 1, 2, 3, 4], scale)

    def p_residual(l, u):
        SS = [k.sb([128, 1280], F32) for _ in range(3)]
        YS = [k.sb([128, 1280], F32) for _ in range(3)]
        QS = [k.sb([128, 1280], F32) for _ in range(2)]
        bSS = [Buf() for _ in range(3)]
        bYS = [Buf() for _ in range(3)]
        bQS = [Buf() for _ in range(2)]
        i = 0
        for c in range(KC):
            for hi, (c0, ncol, tiles) in enumerate(HALVES):
                S, bS, Y, bY, Q, bQ = SS[i % 3], bSS[i % 3], YS[i % 3], bYS[i % 3], QS[i % 2], bQS[i % 2]
                i += 1
                k.dma("sp", S[:, 0:ncol], rows(sT, c)[:, c0:c0 + ncol], writes=[bS])
                k.dma("sp", Y[:, 0:ncol], rows(yT, c)[:, c0:c0 + ncol], writes=[bY])
                k.op("dve", lambda e: e.tensor_tensor(out=Y[:, 0:ncol], in0=Y[:, 0:ncol], in1=RS[:, c0:c0 + ncol], op=ALU.mult), reads=[bY, bRS], writes=[bY])
                for (seg, lo, hi2) in HSEG[hi]:
                    k.op("dve", lambda e: e.scalar_tensor_tensor(out=S[:, lo - c0:hi2 - c0], in0=Y[:, lo - c0:hi2 - c0], scalar=abc(l, u, 2)[:, c, seg:seg + 1],
                                                                 in1=S[:, lo - c0:hi2 - c0], op0=ALU.mult, op1=ALU.add), reads=[bY, bS, bABC], writes=[bS])
                k.dma("sp", rows(sT, c)[:, c0:c0 + ncol], S[:, 0:ncol], reads=[bS])
                k.op("act", lambda e: e.activation(out=Q[:, 0:ncol], in_=S[:, 0:ncol], func=AF.Square), reads=[bS], writes=[bQ])
                ones_acc(Q, bQ, c0, c == 0, c == KC - 1)
        rs_from_psum([0, 1, 2, 3, 4], 1.0 / D)
        phase_end()

    def p_ffn(l, u, wi):
        g = gemm_setup()
        prologue_norm(g, l, u)
        w1 = ffn_w1[l, wi]
        chunks = []
        for j in range(KC):
            chunks += [("g", j), ("v", j)]
        gemm(g, chunks, lambda d: wview(w1, (0 if d[0] == "g" else D) + d[1] * 128), lambda d: 0, KC,
             epi_pair(g, uT, AF.Silu, {}))
        phase_end()
        g = gemm_setup()
        prologue_load(g, uT)
        w2 = ffn_w2[l, wi]
        gemm(g, [("y", j, j == 0) for j in range(KC)], lambda d: wview(w2, d[1] * 128), lambda d: 0, KC, epi_y(g))
        rs_finalize(1.0 / D)
        phase_end()
        p_residual(l, u)

    def p_mix_a(l):
        g = gemm_setup(4, 0)
        prologue_norm(g, l, 1)
        W = w_in[l]
        TK = [k.sb([128, 4, 128], BF16) for _ in range(2)]
        bTK = [Buf(), Buf()]
        tki = [0]

        def epi_tok(dst, ncolsdst):
            dv = dst.rearrange("(tt p) f -> p tt f", p=128)

            def f(desc, hi, banks):
                j = desc[1]
                c0, ncol, _ = HALVES[hi]
                S, bS = stg(g)
                src = k.pflat(banks, ncol)
                k.op("act", lambda e: e.copy(out=S[:, 0:ncol], in_=src), reads=[bps[b] for b in banks], writes=[bS])
                ntt = ncol // 128
                for t4 in range(0, ntt, 4):
                    nq = min(4, ntt - t4)
                    b = k.banks(1)[0]

                    def tr(e):
                        ins = None
                        for q in range(nq):
                            ins = e.transpose(ps[:, b, q * 128:(q + 1) * 128], S[:, (t4 + q) * 128:(t4 + q + 1) * 128], IDN[:, :])
                        return ins
                    k.op("pe", tr, reads=[bS, bIDN], writes=[bps[b]])
                    Tk, bT = TK[tki[0] % 2], bTK[tki[0] % 2]
                    tki[0] += 1
                    k.op("dve", lambda e: e.tensor_copy(out=Tk[:, 0:nq, :].rearrange("p a b -> p (a b)"), in_=ps[:, b, 0:nq * 128]), reads=[bps[b]], writes=[bT])
                    tt0 = c0 // 128 + t4
                    k.dma("sp", dv[:, tt0:tt0 + nq, j * 128:(j + 1) * 128], Tk[:, 0:nq, :], reads=[bT])
            return f

        e_F = epi_store(g, zF)
        e_HQ = epi_store(g, zHQ)
        e_AQ = epi_store(g, zAQ)
        e_K = epi_store(g, zK)
        e_SC = epi_store(g, zSC)
        e_HG = epi_store(g, zHG, AF.Sigmoid)
        e_GT = epi_store(g, zGATE, AF.Sigmoid)
        e_GLU = epi_pair(g, zGLU, AF.Sigmoid, {})
        e_I = epi_tok(vI, 1024)
        e_V = epi_tok(vV, 256)
        chunks = []
        for j in range(16):
            chunks.append(("F", j, j, e_F))
        for j in range(8):
            chunks.append(("I", j, 16 + j, e_I))
        for j in range(2):
            chunks.append(("K", j, 24 + j, e_K))
        for j in range(2):
            chunks.append(("V", j, 26 + j, e_V))
        for j in range(8):
            chunks.append(("HQ", j, 28 + j, e_HQ))
        for j in range(8):
            chunks.append(("HG", j, 36 + j, e_HG))
        for j in range(8):
            chunks.append(("AQ", j, 44 + j, e_AQ))
        for j in range(24):
            chunks.append(("SC", j, 52 + j, e_SC))
        for j in range(8):
            chunks.append(("g", j, 84 + j, e_GLU))
            chunks.append(("v", j, 76 + j, e_GLU))
        for j in range(128):
            chunks.append(("GT", j, 92 + j, e_GT))
        gemm(g, chunks, lambda d: wview(W, d[2] * 128), lambda d: 0, KC, lambda d, hi, banks: d[3](d, hi, banks))
        phase_end()

    def p_hgrn(l):
        CHM = k.sb([128, T], F32)
        TRI = k.sb([32, 2, 512], F32)
        bC = Buf()
        k.dma("sp", CHM[:, :], chm_in, writes=[bC])
        k.dma("sp", TRI[:, :, :], tri_in, writes=[bC])
        T1, T2, T3, T4, T5, Q, O = [k.sb([128, T], F32) for _ in range(7)]
        b1, b2, b3, b4, b5, bQ, bO = [Buf() for _ in range(7)]
        QT = k.sb([128, T], BF16)
        KD = k.sb([128, T], BF16)
        bQT, bKD = Buf(), Buf()
        KOT = k.sb([32, 72, 128], BF16)
        VT = k.sb([32, 72, 128], BF16)
        SALL = k.sb([128, 72, 128], BF16)
        bKOT, bVT, bSALL = Buf(), Buf(), Buf()
        S = k.sb([128, 128], F32)
        DEC = k.sb([128, 72], F32)
        SCM = [k.sb([32, 512], BF16) for _ in range(2)]
        bS, bDEC = Buf(), Buf()
        bSCM = [Buf(), Buf()]
        YB = k.sb([128, T], BF16)
        bYB = Buf()

        def v3(ap):
            return ap[:, :].rearrange("p (c i) -> p c i", i=32)
        for h in range(8):
            k.dma("sp", Q[:, :], rows(zHQ, h), writes=[bQ])
            k.dma("sp", VT[:, :, :], vI[:, h * 128:(h + 1) * 128].rearrange("(c i) d -> i c d", i=32), writes=[bVT])
            for d in range(2):
                lb = LB[:, l, d * 8 + h:d * 8 + h + 1]
                omlb = OMLB[:, l, d * 8 + h:d * 8 + h + 1]
                k.dma("sp", T1[:, :], rows(zF, d * 8 + h), writes=[b1])
                k.op("act", lambda e: e.activation(out=T1[:, :], in_=T1[:, :], func=AF.Exp, scale=-1.0), reads=[b1], writes=[b1])
                k.op("dve", lambda e: e.tensor_scalar(out=T1[:, :], in0=T1[:, :], scalar1=1.0, scalar2=None, op0=ALU.add), reads=[b1], writes=[b1])
                k.op("dve", lambda e: e.reciprocal(out=T1[:, :], in_=T1[:, :]), reads=[b1], writes=[b1])
                k.op("dve", lambda e: e.tensor_scalar(out=T1[:, :], in0=T1[:, :], scalar1=omlb, scalar2=lb, op0=ALU.mult, op1=ALU.add), reads=[b1, bSM], writes=[b1])
                k.op("act", lambda e: e.activation(out=T2[:, :], in_=T1[:, :], func=AF.Ln), reads=[b1], writes=[b2])
                k.op("dve", lambda e: e.tensor_scalar(out=T3[:, :], in0=T1[:, :], scalar1=-1.0, scalar2=1.0, op0=ALU.mult, op1=ALU.add), reads=[b1], writes=[b3])
                k.op("dve", lambda e: e.tensor_tensor_scan(out=T4[:, :], data0=CHM[:, :], data1=T2[:, :], initial=0.0, op0=ALU.mult, op1=ALU.add), reads=[bC, b2], writes=[b4])
                totb = v3(T4)[:, :, 31:32].broadcast_to([128, 72, 32])
                if d == 0:
                    C, bCc = T4, b4
                else:
                    k.op("dve", lambda e: e.tensor_tensor(out=v3(T1), in0=totb, in1=v3(T4), op=ALU.subtract), reads=[b4], writes=[b1])
                    k.op("dve", lambda e: e.tensor_tensor(out=T1[:, :], in0=T1[:, :], in1=T2[:, :], op=ALU.add), reads=[b1, b2], writes=[b1])
                    C, bCc = T1, b1
                k.op("act", lambda e: e.activation(out=T5[:, :], in_=C[:, :], func=AF.Exp), reads=[bCc], writes=[b5])
                k.op("dve", lambda e: e.tensor_tensor(out=QT[:, :], in0=Q[:, :], in1=T5[:, :], op=ALU.mult), reads=[bQ, b5], writes=[bQT])
                k.op("act", lambda e: e.activation(out=T5[:, :], in_=C[:, :], func=AF.Exp, scale=-1.0), reads=[bCc, bQT], writes=[b5])
                k.op("dve", lambda e: e.tensor_tensor(out=KD[:, :], in0=T3[:, :], in1=T5[:, :], op=ALU.mult), reads=[b3, b5], writes=[bKD])
                k.op("dve", lambda e: e.tensor_tensor(out=v3(T5), in0=totb, in1=v3(C), op=ALU.subtract), reads=[b4, bCc, bKD], writes=[b5])
                k.op("act", lambda e: e.activation(out=T5[:, :], in_=T5[:, :], func=AF.Exp), reads=[b5], writes=[b5])
                k.op("dve", lambda e: e.tensor_tensor(out=T5[:, :], in0=T3[:, :], in1=T5[:, :], op=ALU.mult), reads=[b3, b5], writes=[b5])
                k.op("act", lambda e: e.activation(out=DEC[:, :], in_=v3(T4)[:, :, 31], func=AF.Exp), reads=[b4], writes=[bDEC])
                for c4 in range(18):
                    b = k.banks(1)[0]

                    def tr(e):
                        ins = None
                        for q in range(4):
                            c = c4 * 4 + q
                            ins = e.transpose(ps[0:32, b, q * 128:(q + 1) * 128], T5[:, c * 32:(c + 1) * 32], IDN[:, :])
                        return ins
                    k.op("pe", tr, reads=[b5, bIDN], writes=[bps[b]])
                    k.op("act", lambda e: e.copy(out=KOT[:, c4 * 4:(c4 + 1) * 4, :].rearrange("p a b -> p (a b)"), in_=ps[0:32, b, :]), reads=[bps[b]], writes=[bKOT])
                order = (list(range(64, 72)) + list(range(64))) if d == 0 else (list(range(71, 63, -1)) + list(range(63, -1, -1)))
                k.op("dve", lambda e: e.memset(S[:, :], 0.0), writes=[bS])
                for c in order:
                    k.op("act", lambda e: e.copy(out=SALL[:, c, :], in_=S[:, :]), reads=[bS], writes=[bSALL])
                    b = k.banks(1)[0]
                    k.op("pe", lambda e: e.matmul(ps[:, b, 0:128], lhsT=KOT[:, c, :], rhs=VT[:, c, :], start=True, stop=True), reads=[bKOT, bVT], writes=[bps[b]])
                    k.op("dve", lambda e: e.scalar_tensor_tensor(out=S[:, :], in0=S[:, :], scalar=DEC[:, c:c + 1], in1=ps[:, b, 0:128], op0=ALU.mult, op1=ALU.add),
                         reads=[bS, bDEC, bps[b]], writes=[bS])
                for g0 in range(0, 72, 16):
                    ng = min(16, 72 - g0)
                    b = k.banks(1)[0]

                    def sc(e):
                        ins = None
                        for ci in range(ng):
                            c = g0 + ci
                            ins = e.matmul(ps[0:32, b, ci * 32:(ci + 1) * 32], lhsT=KD[:, c * 32:(c + 1) * 32], rhs=QT[:, c * 32:(c + 1) * 32], start=True, stop=True)
                        return ins
                    k.op("pe", sc, reads=[bKD, bQT], writes=[bps[b]])
                    M, bM = SCM[(g0 // 16) % 2], bSCM[(g0 // 16) % 2]
                    k.op("dve", lambda e: e.tensor_tensor(out=M[:, 0:ng * 32], in0=ps[0:32, b, 0:ng * 32], in1=TRI[:, d, 0:ng * 32], op=ALU.mult), reads=[bps[b], bC], writes=[bM])
                    b2_ = k.banks(1)[0]

                    def om(e):
                        ins = None
                        for ci in range(ng):
                            c = g0 + ci
                            e.matmul(ps[:, b2_, ci * 32:(ci + 1) * 32], lhsT=VT[:, c, :], rhs=M[:, ci * 32:(ci + 1) * 32], start=True, stop=False)
                            ins = e.matmul(ps[:, b2_, ci * 32:(ci + 1) * 32], lhsT=SALL[:, c, :], rhs=QT[:, c * 32:(c + 1) * 32], start=False, stop=True)
                        return ins
                    k.op("pe", om, reads=[bVT, bM, bSALL, bQT], writes=[bps[b2_]])
                    if d == 0:
                        k.op("act", lambda e: e.copy(out=O[:, g0 * 32:(g0 + ng) * 32], in_=ps[:, b2_, 0:ng * 32]), reads=[bps[b2_]], writes=[bO])
                    else:
                        k.op("dve", lambda e: e.tensor_tensor(out=O[:, g0 * 32:(g0 + ng) * 32], in0=O[:, g0 * 32:(g0 + ng) * 32], in1=ps[:, b2_, 0:ng * 32], op=ALU.add), reads=[bps[b2_], bO], writes=[bO])
            if "oDbg" in dbg:
                k.dma("sp", rows(oDbg, h), O[:, :], reads=[bO])
            k.op("act", lambda e: e.activation(out=T2[:, :], in_=O[:, :], func=AF.Square), reads=[bO], writes=[b2])
            for hi, (c0, ncol, tiles) in enumerate(HALVES):
                ones_acc(T2[:, c0:c0 + ncol], b2, c0, True, True)
            k.op("act", lambda e: e.activation(out=T3[:, :], in_=k.pflat([0, 1, 2, 3, 4], T), func=AF.Sqrt, scale=1.0 / 128, bias=EPS), reads=[bps[i] for i in range(5)], writes=[b3])
            k.op("dve", lambda e: e.reciprocal(out=T3[:, :], in_=T3[:, :]), reads=[b3], writes=[b3])
            k.op("dve", lambda e: e.tensor_tensor(out=T3[:, :], in0=T3[:, :], in1=O[:, :], op=ALU.mult), reads=[b3, bO], writes=[b3])
            k.dma("sp", T4[:, :], rows(zHG, h), writes=[b4])
            k.op("dve", lambda e: e.scalar_tensor_tensor(out=YB[:, :], in0=T3[:, :], scalar=HGN[:, l:l + 1], in1=T4[:, :], op0=ALU.mult, op1=ALU.mult), reads=[b3, b4, bSM], writes=[bYB])
            k.dma("sp", rows(brT, h), YB[:, :], reads=[bYB])
        phase_end()

    def p_attn(l):
        ROT = k.sb([128, 128], F32)
        COS = k.sb([128, TL], F32)
        SIN = k.sb([128, TL], F32)
        bR = Buf()
        k.dma("sp", ROT[:, :], rot_in, writes=[bR])
        k.dma("sp", COS[:, :], cos_in, writes=[bR])
        k.dma("sp", SIN[:, :], sin_in, writes=[bR])
        X, XN, R, TM = [k.sb([128, T], F32) for _ in range(4)]
        bX, bXN, bRr, bTM = [Buf() for _ in range(4)]
        KT = k.sb([128, T], BF16)
        QT = k.sb([128, T], BF16)
        VT = k.sb([128, 18, 128], BF16)
        bKT, bQT, bVT = Buf(), Buf(), Buf()
        PT = [k.sb([128, 512], BF16) for _ in range(4)]
        bPT = [Buf() for _ in range(4)]
        RD = k.sb([128, 512], F32)
        AO = [k.sb([128, 512], BF16) for _ in range(2)]
        bRD = Buf()
        bAO = [Buf(), Buf()]

        def qk_prep(src, gcol, DST, bDST):
            k.dma("sp", X[:, :], src, writes=[bX])
            k.op("act", lambda e: e.activation(out=TM[:, :], in_=X[:, :], func=AF.Square), reads=[bX], writes=[bTM])
            for hi, (c0, ncol, tiles) in enumerate(HALVES):
                ones_acc(TM[:, c0:c0 + ncol], bTM, c0, True, True)
            k.op("act", lambda e: e.activation(out=R[:, :], in_=k.pflat([0, 1, 2, 3, 4], T), func=AF.Sqrt, scale=1.0 / 128, bias=EPS), reads=[bps[i] for i in range(5)], writes=[bRr])
            k.op("dve", lambda e: e.reciprocal(out=R[:, :], in_=R[:, :]), reads=[bRr], writes=[bRr])
            k.op("dve", lambda e: e.scalar_tensor_tensor(out=XN[:, :], in0=X[:, :], scalar=QKN[:, gcol:gcol + 1], in1=R[:, :], op0=ALU.mult, op1=ALU.mult), reads=[bX, bRr, bSM], writes=[bXN])

            def mm(e):
                ins = None
                for t in range(4):
                    ins = e.matmul(ps[:, t, :], lhsT=ROT[:, :], rhs=XN[:, t * 512:(t + 1) * 512], start=True, stop=True)
                return ins
            k.op("pe", mm, reads=[bXN, bR], writes=[bps[i] for i in range(4)])
            k.op("dve", lambda e: e.tensor_tensor(out=TM[:, 0:TL], in0=k.pflat([0, 1, 2, 3], TL), in1=SIN[:, :], op=ALU.mult), reads=[bps[i] for i in range(4)] + [bR], writes=[bTM])
            k.op("dve", lambda e: e.tensor_tensor(out=R[:, 0:TL], in0=XN[:, 0:TL], in1=COS[:, :], op=ALU.mult), reads=[bXN, bR, bRr], writes=[bRr])
            k.op("dve", lambda e: e.tensor_tensor(out=DST[:, 0:TL], in0=R[:, 0:TL], in1=TM[:, 0:TL], op=ALU.add), reads=[bRr, bTM], writes=[bDST])
            k.op("act", lambda e: e.copy(out=DST[:, TL:T], in_=XN[:, TL:T]), reads=[bXN], writes=[bDST])

        it = 0
        pi = 0
        for gk in range(2):
            qk_prep(rows(zK, gk), 2 * l + 1, KT, bKT)
            k.dma("sp", VT[:, :, :], vV[:, gk * 128:(gk + 1) * 128].rearrange("(tt p) d -> p tt d", p=128), writes=[bVT])
            for hq4 in range(4):
                hq = gk * 4 + hq4
                qk_prep(rows(zAQ, hq), 2 * l, QT, bQT)
                for (q0, qn, kts) in [(0, 512, range(18)), (512, 512, range(18)), (1024, 512, range(18)), (1536, 512, range(18)), (2048, 256, (16, 17))]:
                    bx, by = (0, 1) if it % 2 == 0 else (2, 3)
                    it += 1
                    kts = list(kts)
                    for ki, kt in enumerate(kts):
                        bs = 4 + (pi % 4)
                        P, bP = PT[pi % 4], bPT[pi % 4]
                        pi += 1
                        k.op("pe", lambda e: e.matmul(ps[:, bs, 0:qn], lhsT=KT[:, kt * 128:(kt + 1) * 128], rhs=QT[:, q0:q0 + qn], start=True, stop=True), reads=[bKT, bQT], writes=[bps[bs]])
                        k.op("act", lambda e: e.activation(out=P[:, 0:qn], in_=ps[:, bs, 0:qn], func=AF.Exp, scale=128.0 ** -0.5), reads=[bps[bs]], writes=[bP])

                        def ov(e):
                            e.matmul(ps[:, bx, 0:qn], lhsT=VT[:, kt, :], rhs=P[:, 0:qn], start=(ki == 0), stop=(ki == len(kts) - 1))
                            return e.matmul(ps[:, by, 0:qn], lhsT=ONES_B[:, :], rhs=P[:, 0:qn], start=(ki == 0), stop=(ki == len(kts) - 1))
                        k.op("pe", ov, reads=[bVT, bP, bONES], writes=[bps[bx], bps[by]])
                    k.op("dve", lambda e: e.reciprocal(out=RD[:, 0:qn], in_=ps[:, by, 0:qn]), reads=[bps[by]], writes=[bRD])
                    A, bA = AO[it % 2], bAO[it % 2]
                    k.op("dve", lambda e: e.tensor_tensor(out=A[:, 0:qn], in0=ps[:, bx, 0:qn], in1=RD[:, 0:qn], op=ALU.mult), reads=[bps[bx], bRD], writes=[bA])
                    k.dma("sp", rows(brT, 8 + hq)[:, q0:q0 + qn], A[:, 0:qn], reads=[bA])
        phase_end()

    def p_sconv(l):
        SCW = k.sb([128, 8, 3], F32)
        bW = Buf()
        k.dma("sp", SCW[:, :, :], scwT[:, l], writes=[bW])
        BG, CG, U_, OUT = [[k.sb([128, T], F32) for _ in range(2)] for _ in range(4)]
        bBG, bCG, bU, bOUT = [[Buf(), Buf()] for _ in range(4)]
        YB = [k.sb([128, T], BF16) for _ in range(2)]
        bYB = [Buf(), Buf()]
        for cc in range(8):
            i = cc % 2
            k.dma("sp", BG[i][:, :], rows(zSC, cc), writes=[bBG[i]])
            k.dma("sp", CG[i][:, :], rows(zSC, 8 + cc), writes=[bCG[i]])
            k.dma("sp", U_[i][:, :], rows(zSC, 16 + cc), writes=[bU[i]])
            M, o = CG[i], OUT[i]
            k.op("dve", lambda e: e.tensor_tensor(out=M[:, :], in0=M[:, :], in1=U_[i][:, :], op=ALU.mult), reads=[bCG[i], bU[i]], writes=[bCG[i]])
            k.op("dve", lambda e: e.tensor_scalar(out=o[:, :], in0=M[:, :], scalar1=SCW[:, cc, 1:2], scalar2=None, op0=ALU.mult), reads=[bCG[i], bW], writes=[bOUT[i]])
            for (lo, hi2) in [(0, TL), (TL, T)]:
                k.op("dve", lambda e: e.scalar_tensor_tensor(out=o[:, lo + 1:hi2], in0=M[:, lo:hi2 - 1], scalar=SCW[:, cc, 0:1], in1=o[:, lo + 1:hi2], op0=ALU.mult, op1=ALU.add), reads=[bCG[i], bW, bOUT[i]], writes=[bOUT[i]])
                k.op("dve", lambda e: e.scalar_tensor_tensor(out=o[:, lo:hi2 - 1], in0=M[:, lo + 1:hi2], scalar=SCW[:, cc, 2:3], in1=o[:, lo:hi2 - 1], op0=ALU.mult, op1=ALU.add), reads=[bCG[i], bW, bOUT[i]], writes=[bOUT[i]])
            k.op("dve", lambda e: e.tensor_tensor(out=YB[i][:, :], in0=o[:, :], in1=BG[i][:, :], op=ALU.mult), reads=[bOUT[i], bBG[i]], writes=[bYB[i]])
            k.dma("sp", rows(brT, 16 + cc), YB[i][:, :], reads=[bYB[i]])
        phase_end()

    def p_conf(l):
        CFW = k.sb([128, 8, 31], F32)
        CFB = k.sb([128, 8], F32)
        LNG = k.sb([128, 8], F32)
        LNB = k.sb([128, 8], F32)
        bW = Buf()
        k.dma("sp", CFW[:, :, :], cfwT[:, l], writes=[bW])
        k.dma("sp", CFB[:, :], cfbT[:, l], writes=[bW])
        k.dma("sp", LNG[:, :], lngT[:, l], writes=[bW])
        k.dma("sp", LNB[:, :], lnbT[:, l], writes=[bW])
        PADL = [k.sb([128, TL + 30], F32) for _ in range(2)]
        PADC = [k.sb([128, TC + 30], F32) for _ in range(2)]
        bPAD = [Buf(), Buf()]
        for i in range(2):
            k.op("dve", lambda e: e.memset(PADL[i][:, :], 0.0), writes=[bPAD[i]])
            k.op("dve", lambda e: e.memset(PADC[i][:, :], 0.0), writes=[bPAD[i]])
        U = [k.sb([128, T], F32) for _ in range(8)]
        bU = [Buf() for _ in range(8)]
        MEAN, RSD, TMP = [k.sb([128, T], F32) for _ in range(3)]
        bMEAN, bRSD, bTMP = Buf(), Buf(), Buf()
        YB = [k.sb([128, T], BF16) for _ in range(2)]
        bYB = [Buf(), Buf()]
        for cc in range(8):
            i = cc % 2
            k.dma("sp", PADL[i][:, 15:15 + TL], rows(zGLU, cc)[:, 0:TL], writes=[bPAD[i]])
            k.dma("sp", PADC[i][:, 15:15 + TC], rows(zGLU, cc)[:, TL:T], writes=[bPAD[i]])
            for (P, lo, n) in [(PADL[i], 0, TL), (PADC[i], TL, TC)]:
                k.op("dve", lambda e: e.tensor_scalar(out=U[cc][:, lo:lo + n], in0=P[:, 0:n], scalar1=CFW[:, cc, 0:1], scalar2=CFB[:, cc:cc + 1], op0=ALU.mult, op1=ALU.add), reads=[bPAD[i], bW], writes=[bU[cc]])
                for tp in range(1, 31):
                    k.op("dve", lambda e: e.scalar_tensor_tensor(out=U[cc][:, lo:lo + n], in0=P[:, tp:tp + n], scalar=CFW[:, cc, tp:tp + 1], in1=U[cc][:, lo:lo + n], op0=ALU.mult, op1=ALU.add), reads=[bPAD[i], bW, bU[cc]], writes=[bU[cc]])
        for cc in range(8):
            for hi, (c0, ncol, tiles) in enumerate(HALVES):
                ones_acc(U[cc][:, c0:c0 + ncol], bU[cc], c0, cc == 0, cc == 7)
        k.op("act", lambda e: e.activation(out=MEAN[:, :], in_=k.pflat([0, 1, 2, 3, 4], T), func=AF.Copy, scale=1.0 / 1024), reads=[bps[i] for i in range(5)], writes=[bMEAN])
        for cc in range(8):
            k.op("act", lambda e: e.activation(out=TMP[:, :], in_=U[cc][:, :], func=AF.Square), reads=[bU[cc]], writes=[bTMP])
            for hi, (c0, ncol, tiles) in enumerate(HALVES):
                ones_acc(TMP[:, c0:c0 + ncol], bTMP, c0, cc == 0, cc == 7)
        k.op("act", lambda e: e.activation(out=TMP[:, :], in_=MEAN[:, :], func=AF.Square), reads=[bMEAN], writes=[bTMP])
        k.op("dve", lambda e: e.scalar_tensor_tensor(out=RSD[:, :], in0=k.pflat([0, 1, 2, 3, 4], T), scalar=1.0 / 1024, in1=TMP[:, :], op0=ALU.mult, op1=ALU.subtract), reads=[bps[i] for i in range(5)] + [bTMP], writes=[bRSD])
        k.op("act", lambda e: e.activation(out=RSD[:, :], in_=RSD[:, :], func=AF.Sqrt, bias=EPS), reads=[bRSD], writes=[bRSD])
        k.op("dve", lambda e: e.reciprocal(out=RSD[:, :], in_=RSD[:, :]), reads=[bRSD], writes=[bRSD])
        for cc in range(8):
            i = cc % 2
            k.op("dve", lambda e: e.tensor_tensor(out=U[cc][:, :], in0=U[cc][:, :], in1=MEAN[:, :], op=ALU.subtract), reads=[bU[cc], bMEAN], writes=[bU[cc]])
            k.op("dve", lambda e: e.tensor_tensor(out=U[cc][:, :], in0=U[cc][:, :], in1=RSD[:, :], op=ALU.mult), reads=[bU[cc], bRSD], writes=[bU[cc]])
            k.op("act", lambda e: e.activation(out=YB[i][:, :], in_=U[cc][:, :], func=AF.Silu, scale=LNG[:, cc:cc + 1], bias=LNB[:, cc:cc + 1]), reads=[bU[cc], bW], writes=[bYB[i]])
            k.dma("sp", rows(brT, 24 + cc), YB[i][:, :], reads=[bYB[i]])
        phase_end()

    def p_mix_c(l):
        g = gemm_setup(nstage=0, nbf=2)
        prologue_load(g, brT)
        ACM = [k.sb([128, 1280], F32) for _ in range(2)]
        bACM = [Buf(), Buf()]
        GT = [k.sb([128, 1280], F32) for _ in range(2)]
        bGT = [Buf(), Buf()]
        gi = [0]

        def epi(desc, hi, banks):
            _, j, i = desc
            c0, ncol, _ = HALVES[hi]
            src = k.pflat(banks, ncol)
            Gt, bG = GT[gi[0] % 2], bGT[gi[0] % 2]
            gi[0] += 1
            k.dma("sp", Gt[:, 0:ncol], rows(zGATE, i * 32 + j)[:, c0:c0 + ncol], writes=[bG])
            A, bA = ACM[hi], bACM[hi]
            if i == 0:
                k.op("dve", lambda e: e.tensor_tensor(out=A[:, 0:ncol], in0=src, in1=Gt[:, 0:ncol], op=ALU.mult), reads=[bG] + [bps[b] for b in banks], writes=[bA])
            else:
                k.op("dve", lambda e: e.tensor_tensor(out=Gt[:, 0:ncol], in0=src, in1=Gt[:, 0:ncol], op=ALU.mult), reads=[bG] + [bps[b] for b in banks], writes=[bG])
                k.op("dve", lambda e: e.tensor_tensor(out=A[:, 0:ncol], in0=A[:, 0:ncol], in1=Gt[:, 0:ncol], op=ALU.add), reads=[bG, bA], writes=[bA])
            if i == 3:
                U, bU = stgb(g)
                k.op("act", lambda e: e.copy(out=U[:, 0:ncol], in_=A[:, 0:ncol]), reads=[bA], writes=[bU])
                k.dma("sp", rows(accT, j)[:, c0:c0 + ncol], U[:, 0:ncol], reads=[bU])
        chunks = [("m", j, i) for j in range(KC) for i in range(4)]
        gemm(g, chunks, lambda d: wview(w_branch[l, d[2]], d[1] * 128, nkc=8), lambda d: d[2] * 8, 8, epi)
        phase_end()

    def p_mix_d(l):
        g = gemm_setup()
        prologue_load(g, accT)
        wo = w_out[l]
        gemm(g, [("y", j, j == 0) for j in range(KC)], lambda d: wview(wo, d[1] * 128), lambda d: 0, KC, epi_y(g))
        rs_finalize(1.0 / D)
        phase_end()
        p_residual(l, 1)

    def p_transpose_out():
        ST = [k.sb([128, KC, 128], F32) for _ in range(2)]
        bST = [Buf(), Buf()]
        OT = [k.sb([128, D], F32) for _ in range(2)]
        bOT = [Buf(), Buf()]
        sTv = sT.rearrange("(c p) t -> p c t", p=128)
        for tt in range(16):
            S, bS, O, bO = ST[tt % 2], bST[tt % 2], OT[tt % 2], bOT[tt % 2]
            k.dma("sp", S[:, :, :], sTv[:, :, tt * 128:(tt + 1) * 128], writes=[bS])
            for c4 in range(8):
                b = k.banks(1)[0]

                def tr(e):
                    ins = None
                    for q in range(4):
                        ins = e.transpose(ps[:, b, q * 128:(q + 1) * 128], S[:, c4 * 4 + q, :], IDN[:, :])
                    return ins
                k.op("pe", tr, reads=[bS, bIDN], writes=[bps[b]])
                if c4 % 2 == 0:
                    k.op("act", lambda e: e.copy(out=O[:, c4 * 512:(c4 + 1) * 512], in_=ps[:, b, :]), reads=[bps[b]], writes=[bO])
                else:
                    k.op("dve", lambda e: e.tensor_copy(out=O[:, c4 * 512:(c4 + 1) * 512], in_=ps[:, b, :]), reads=[bps[b]], writes=[bO])
            k.dma("sp", out_d[tt * 128:(tt + 1) * 128, :], O[:, :], reads=[bO])
        phase_end()

    phases = [("tin", p_transpose_in), ("stats", p_stats), ("mod", p_mod)]
    for l in range(2):
        phases += [("ffn0_%d" % l, lambda l=l: p_ffn(l, 0, 0)),
                   ("mixa_%d" % l, lambda l=l: p_mix_a(l)),
                   ("hgrn_%d" % l, lambda l=l: p_hgrn(l)),
                   ("attn_%d" % l, lambda l=l: p_attn(l)),
                   ("sconv_%d" % l, lambda l=l: p_sconv(l)),
                   ("conf_%d" % l, lambda l=l: p_conf(l)),
                   ("mixc_%d" % l, lambda l=l: p_mix_c(l)),
                   ("mixd_%d" % l, lambda l=l: p_mix_d(l)),
                   ("ffn1_%d" % l, lambda l=l: p_ffn(l, 2, 1))]
    for name, fn in phases:
        fn()
        if stop_after == name:
            break
    p_transpose_out()
    k.barrier()
    return nc


def host_consts():
    idn = np.eye(128, dtype=np.float32)
    rot = np.zeros((128, 128), np.float32)
    for base in (0, 64):
        for m in range(32):
            rot[base + m + 32, base + m] = -1.0
            rot[base + m, base + m + 32] = 1.0
    inv = (10000.0 ** (-np.arange(0, 64, 2, dtype=np.float32) / 64)).astype(np.float32)
    t = np.arange(TL)
    ang_r = (t // 64).astype(np.float32)[None, :] * inv[:, None]
    ang_c = (t % 64).astype(np.float32)[None, :] * inv[:, None]
    ang = np.concatenate([ang_r, ang_r, ang_c, ang_c], 0).astype(np.float32)
    chm = np.ones((128, T), np.float32)
    chm[:, ::32] = 0.0
    j = np.arange(32)[:, None]
    i = np.arange(32)[None, :]
    tri = np.stack([np.tile((j <= i).astype(np.float32), (1, 16)), np.tile((j >= i).astype(np.float32), (1, 16))], 1)
    return dict(idn=idn, rot=rot, cosT=np.cos(ang).astype(np.float32), sinT=np.sin(ang).astype(np.float32), chm=chm, tri=np.ascontiguousarray(tri))


def fm(v):
    v = np.asarray(v, np.float32)
    lead = v.shape[:-1]
    a = v.reshape(lead + (v.shape[-1] // 128, 128))
    return np.ascontiguousarray(np.moveaxis(a, -1, 0))


def make_in_maps(inp, cores):
    cs = host_consts()
    shared = dict(cs)
    shared["w_mod"] = np.ascontiguousarray(inp["w_mod"], np.float32)
    shared["bmodT"] = fm(inp["b_mod"])
    shared["ngT"] = fm(inp["norm_g"])
    shared["ffn_w1"] = np.ascontiguousarray(inp["ffn_w1"], np.float32)
    shared["ffn_w2"] = np.ascontiguousarray(inp["ffn_w2"], np.float32)
    shared["w_in"] = np.ascontiguousarray(inp["w_in"], np.float32)
    lbl = np.asarray(inp["hgrn_lb_logits"], np.float32).reshape(2, 2, 8, 128)
    shared["lblT"] = np.ascontiguousarray(np.transpose(lbl, (3, 0, 1, 2)).reshape(128, 2, 16))
    shared["hgnT"] = np.ascontiguousarray(np.asarray(inp["hgrn_norm_g"], np.float32).T)
    shared["qknT"] = np.ascontiguousarray(np.asarray(inp["qk_norm_g"], np.float32).reshape(4, 128).T)
    shared["scwT"] = np.ascontiguousarray(np.transpose(np.asarray(inp["short_conv_w"], np.float32).reshape(2, 3, 8, 128), (3, 0, 2, 1)))
    shared["cfwT"] = np.ascontiguousarray(np.transpose(np.asarray(inp["conf_dw_w"], np.float32).reshape(2, 31, 8, 128), (3, 0, 2, 1)))
    shared["cfbT"] = fm(inp["conf_dw_b"])
    shared["lngT"] = fm(inp["conf_ln_g"])
    shared["lnbT"] = fm(inp["conf_ln_b"])
    shared["w_branch"] = np.ascontiguousarray(inp["w_branch"], np.float32)
    shared["w_out"] = np.ascontiguousarray(inp["w_out"], np.float32)
    cctx = fm(inp["c_ctx"])
    maps = []
    for b in cores:
        m = dict(shared)
        m["x"] = np.ascontiguousarray(inp["x"][b], np.float32)
        m["ctx"] = np.ascontiguousarray(inp["ctx"][b], np.float32)
        m["cT"] = np.ascontiguousarray(np.stack([fm(inp["c"][b]), cctx], -1))
        maps.append(m)
    return maps


def kernel(**inputs):
    nc = build()
    in_maps = make_in_maps(inputs, list(range(8)))
    res = run_bass_kernel_spmd(nc, in_maps, core_ids=list(range(8)))
    return np.stack([r["out"] for r in res.results], 0).astype(np.float32)
```

```python
import numpy as np
import concourse.bass as bass
import concourse.mybir as mybir
from concourse.bass_utils import run_bass_kernel_spmd

F32 = mybir.dt.float32
BF16 = mybir.dt.bfloat16
AF = mybir.ActivationFunctionType
ALU = mybir.AluOpType

D = 4096
TL = 2048
TC = 256
T = TL + TC
KC = 32
EPS = 1e-6
NIN = 28160
HALVES_FULL = [(0, 1024, [(0, 512), (512, 512)]), (1024, 1280, [(1024, 512), (1536, 512), (2048, 256)])]
HSEG_FULL = [[(0, 0, 1024)], [(0, 1024, 2048), (1, 2048, 2304)]]
HALVES_LAT = [(0, 1024, [(0, 512), (512, 512)]), (1024, 1024, [(1024, 512), (1536, 512)])]
HSEG_LAT = [[(0, 0, 1024)], [(0, 1024, 2048)]]
HALVES = list(HALVES_FULL)
HSEG = list(HSEG_FULL)
TOK = {"lat": False}


def set_tokens(lat_only):
    TOK["lat"] = lat_only
    HALVES[:] = HALVES_LAT if lat_only else HALVES_FULL
    HSEG[:] = HSEG_LAT if lat_only else HSEG_FULL


class Buf:
    __slots__ = ("w", "r")

    def __init__(self):
        self.w = None
        self.r = []


class KB:
    NDMA = 24

    def __init__(self):
        self.nc = bass.Bass("TRN2", target_bir_lowering=False)
        nc = self.nc
        self.eng = {"pe": nc.tensor, "act": nc.scalar, "dve": nc.vector, "pool": nc.gpsimd, "sp": nc.sync}
        self.sem = {k: nc.alloc_semaphore("s_" + k) for k in ("pe", "act", "dve", "pool")}
        self.cnt = {k: 0 for k in self.sem}
        self.dsem = [nc.alloc_semaphore("d%d" % i) for i in range(self.NDMA)]
        self.dcnt = [0] * self.NDMA
        self.dnext = 0
        self.seen = {k: {} for k in self.eng}
        self.ARENA = 51712 * 4
        self.arena = nc.alloc_sbuf_tensor("arena", [128, self.ARENA // 4], F32)
        self.sb_off = 0
        self.ps = nc.alloc_psum_tensor("psall", [128, 8, 512], F32)
        self.bps = [Buf() for _ in range(8)]
        self.pnext = 0

    def sb(self, shape, dtype):
        nb = int(np.prod(shape[1:])) * (2 if dtype == BF16 else 4)
        nb = (nb + 63) // 64 * 64
        o = self.sb_off
        self.sb_off += nb
        assert self.sb_off <= self.ARENA, ("sbuf overflow", self.sb_off)
        v = self.arena[:, o // 4:(o + nb) // 4]
        if dtype == BF16:
            v = v.bitcast(BF16)
        n = int(np.prod(shape[1:]))
        v = v[0:shape[0], 0:n]
        if len(shape) == 3:
            v = v.rearrange("p (a b) -> p a b", b=shape[2])
        elif len(shape) == 4:
            v = v.rearrange("p (a b c) -> p a b c", b=shape[2], c=shape[3])
        return v

    def banks(self, n):
        if self.pnext + n > 8:
            self.pnext = 0
        b = list(range(self.pnext, self.pnext + n))
        self.pnext = (self.pnext + n) % 8
        return b

    def pflat(self, banks, ncols):
        return self.ps[:, banks[0]:banks[0] + len(banks), :].rearrange("p a b -> p (a b)")[:, 0:ncols]

    def _wait(self, e, tickets):
        eng = self.eng[e]
        seen = self.seen[e]
        best = {}
        for t in tickets:
            if t is None:
                continue
            s, v, own = t
            if own == "pe" and e == "pe":
                continue
            if seen.get(id(s), 0) >= v:
                continue
            if best.get(id(s), (None, 0))[1] < v:
                best[id(s)] = (s, v)
        for s, v in best.values():
            eng.wait_ge(s, v)
            seen[id(s)] = v

    @staticmethod
    def _deps(reads, writes):
        need = []
        for b in reads:
            need.append(b.w)
        for b in writes:
            need.append(b.w)
            need.extend(b.r)
        return need

    @staticmethod
    def _mark(t, reads, writes):
        for b in reads:
            b.r.append(t)
            if len(b.r) > 48:
                b.r = b.r[-48:]
        for b in writes:
            b.w = t
            b.r = []

    def op(self, e, fn, reads=(), writes=()):
        self._wait(e, self._deps(reads, writes))
        ins = fn(self.eng[e])
        self.cnt[e] += 1
        ins.then_inc(self.sem[e], 1)
        t = (self.sem[e], self.cnt[e], e)
        self._mark(t, reads, writes)
        return t

    def dma(self, q, out, in_, reads=(), writes=()):
        i = self.dnext
        self.dnext = (self.dnext + 1) % self.NDMA
        s = self.dsem[i]
        need = self._deps(reads, writes)
        if self.dcnt[i] > 0:
            need.append((s, self.dcnt[i], "dma"))
        self._wait(q, need)
        self.eng[q].dma_start(out=out, in_=in_).then_inc(s, 16)
        self.dcnt[i] += 16
        t = (s, self.dcnt[i], "dma")
        self._mark(t, reads, writes)
        return t

    def barrier(self):
        alls = [(self.sem[k], self.cnt[k], k + "_b") for k in self.sem if self.cnt[k] > 0]
        alls += [(self.dsem[i], self.dcnt[i], "dma") for i in range(self.NDMA) if self.dcnt[i] > 0]
        for e in self.eng:
            self._wait(e, alls)
        for b in self.bps:
            b.w = None
            b.r = []


def build(stop_after=None, dbg=()):
    set_tokens(False)
    k = KB()
    nc = k.nc
    ps = k.ps
    bps = k.bps

    def din(name, shape, dt=F32):
        return nc.dram_tensor(name, list(shape), dt, kind="ExternalInput").ap()

    def dscr(name, shape, dt=F32):
        kind = "ExternalOutput" if name in dbg else "Internal"
        return nc.dram_tensor(name, list(shape), dt, kind=kind).ap()

    x_in = din("x", [TL, D])
    ctx_in = din("ctx", [TC, D])
    cT_in = din("cT", [128, KC, 2])
    w_mod = din("w_mod", [2, D, 9 * D])
    bmodT = din("bmodT", [128, 2, 288])
    ngT = din("ngT", [128, 2, 6, 32])
    ffn_w1 = din("ffn_w1", [2, 2, D, 2 * D])
    ffn_w2 = din("ffn_w2", [2, 2, D, D])
    w_in = din("w_in", [2, D, NIN])
    lblT = din("lblT", [128, 2, 16])
    hgnT = din("hgnT", [128, 2])
    qknT = din("qknT", [128, 4])
    scwT = din("scwT", [128, 2, 8, 3])
    cfwT = din("cfwT", [128, 2, 8, 31])
    cfbT = din("cfbT", [128, 2, 8])
    lngT = din("lngT", [128, 2, 8])
    lnbT = din("lnbT", [128, 2, 8])
    w_branch = din("w_branch", [2, 4, 1024, D])
    w_out = din("w_out", [2, D, D])
    idn_in = din("idn", [128, 128])
    rot_in = din("rot", [128, 128])
    cos_in = din("cosT", [128, TL])
    sin_in = din("sinT", [128, TL])
    chm_in = din("chm", [128, T])
    tri_in = din("tri", [32, 2, 512])
    out_d = nc.dram_tensor("out", [TL, D], F32, kind="ExternalOutput").ap()

    sT = dscr("sT", [D, T])
    yT = dscr("yT", [D, T])
    uT = dscr("uT", [D, T], BF16)
    zF = dscr("zF", [2048, T])
    zHQ = dscr("zHQ", [1024, T])
    zHG = dscr("zHG", [1024, T])
    zAQ = dscr("zAQ", [1024, T])
    zK = dscr("zK", [256, T])
    zSC = dscr("zSC", [3072, T])
    zGLU = dscr("zGLU", [1024, T])
    zGATE = dscr("zGATE", [4 * D, T])
    vI = dscr("vI", [T, 1024], BF16)
    vV = dscr("vV", [T, 256], BF16)
    brT = dscr("brT", [D, T], BF16)
    accT = dscr("accT", [D, T], BF16)
    oDbg = dscr("oDbg", [1024, T])

    def rows(ap, c):
        return ap[c * 128:(c + 1) * 128, :]

    ONES_F = k.sb([128, 128], F32)
    ONES_B = k.sb([128, 128], BF16)
    IDN = k.sb([128, 128], F32)
    RS = k.sb([128, T], F32)
    ABC = k.sb([128, 18, 32, 2], F32)
    LB = k.sb([128, 2, 16], F32)
    OMLB = k.sb([128, 2, 16], F32)
    HGN = k.sb([128, 2], F32)
    QKN = k.sb([128, 4], F32)
    bONES, bIDN, bRS, bABC, bSM = Buf(), Buf(), Buf(), Buf(), Buf()
    PBASE = k.sb_off

    def abc(l, u, w):
        return ABC[:, (l * 3 + u) * 3 + w, :, :]

    k.op("dve", lambda e: e.memset(ONES_F[:, :], 1.0), writes=[bONES])
    k.op("dve", lambda e: e.memset(ONES_B[:, :], 1.0), writes=[bONES])
    k.dma("sp", IDN[:, :], idn_in, writes=[bIDN])
    k.dma("sp", HGN[:, :], hgnT, writes=[bSM])
    k.dma("sp", QKN[:, :], qknT, writes=[bSM])

    def phase_end():
        k.barrier()
        k.sb_off = PBASE

    def rs_from_psum(banks5, scale):
        ta = TL if TOK["lat"] else T
        src = k.pflat(banks5, ta)
        k.op("act", lambda e: e.activation(out=RS[:, 0:ta], in_=src, func=AF.Sqrt, scale=scale, bias=EPS),
             reads=[bps[b] for b in banks5], writes=[bRS])
        k.op("dve", lambda e: e.reciprocal(out=RS[:, 0:ta], in_=RS[:, 0:ta]), reads=[bRS], writes=[bRS])

    def ones_acc(src_ap, bsrc, col0, first, last, lhs=None):
        hi = 0 if col0 == 0 else 1
        _, ncols, tiles = HALVES[hi]

        def mm(e):
            ins = None
            for (t0, tn) in tiles:
                ins = e.matmul(ps[:, t0 // 512, 0:tn], lhsT=(ONES_F if lhs is None else lhs)[:, :],
                               rhs=src_ap[:, t0 - col0:t0 - col0 + tn], start=first, stop=last)
            return ins
        k.op("pe", mm, reads=[bsrc, bONES], writes=[bps[t0 // 512] for (t0, tn) in tiles])

    def p_transpose_in():
        XT = [k.sb([128, D], F32) for _ in range(2)]
        bXT = [Buf(), Buf()]
        ST = [k.sb([128, KC, 128], F32) for _ in range(2)]
        bST = [Buf(), Buf()]
        sTv = sT.rearrange("(c p) t -> p c t", p=128)
        for tt in range(18):
            src = x_in[tt * 128:(tt + 1) * 128, :] if tt < 16 else ctx_in[(tt - 16) * 128:(tt - 15) * 128, :]
            X, bX = XT[tt % 2], bXT[tt % 2]
            S, bS = ST[tt % 2], bST[tt % 2]
            k.dma("sp", X[:, :], src, writes=[bX])
            for c4 in range(8):
                b = k.banks(1)[0]

                def tr(e):
                    ins = None
                    for q in range(4):
                        c = c4 * 4 + q
                        ins = e.transpose(ps[:, b, q * 128:(q + 1) * 128], X[:, c * 128:(c + 1) * 128], IDN[:, :])
                    return ins
                k.op("pe", tr, reads=[bX, bIDN], writes=[bps[b]])
                eng = "act" if c4 % 2 == 0 else "dve"
                dst = S[:, c4 * 4:(c4 + 1) * 4, :].rearrange("p a b -> p (a b)")
                if eng == "act":
                    k.op("act", lambda e: e.copy(out=dst, in_=ps[:, b, :]), reads=[bps[b]], writes=[bS])
                else:
                    k.op("dve", lambda e: e.tensor_copy(out=dst, in_=ps[:, b, :]), reads=[bps[b]], writes=[bS])
            k.dma("pool", sTv[:, :, tt * 128:(tt + 1) * 128], S[:, :, :], reads=[bS])
        phase_end()

    def p_stats():
        SG = [k.sb([128, 1280], F32) for _ in range(3)]
        bSG = [Buf() for _ in range(3)]
        i = 0
        for c in range(KC):
            for hi, (c0, ncol, tiles) in enumerate(HALVES):
                S, bS = SG[i % 3], bSG[i % 3]
                i += 1
                k.dma("sp", S[:, 0:ncol], rows(sT, c)[:, c0:c0 + ncol], writes=[bS])
                k.op("act", lambda e: e.activation(out=S[:, 0:ncol], in_=S[:, 0:ncol], func=AF.Square), reads=[bS], writes=[bS])
                ones_acc(S, bS, c0, c == 0, c == KC - 1)
        rs_from_psum([0, 1, 2, 3, 4], 1.0 / D)
        phase_end()

    class MS:
        pass

    def mod_alloc(wcols, nwm):
        m = MS()
        m.CT = k.sb([128, KC, 2], F32)
        m.SIL = k.sb([128, KC, 2], BF16)
        m.MODT = k.sb([128, 288, 2], F32)
        m.BM = k.sb([128, 288], F32)
        m.NG = k.sb([128, 6, 32], F32)
        m.MR = [k.sb([2, 512], F32) for _ in range(2)]
        m.wcols = wcols
        m.WM = [k.sb([128, KC, wcols], BF16) for _ in range(nwm)]
        return m

    def mod_gen(l, m):
        bCT, bSIL, bMODT, bBM, bNG = Buf(), Buf(), Buf(), Buf(), Buf()
        bMR = [Buf(), Buf()]
        nwm = len(m.WM)
        bWM = [Buf() for _ in range(nwm)]
        k.dma("sp", m.CT[:, :, :], cT_in, writes=[bCT])
        k.dma("sp", m.BM[:, :], bmodT[:, l], writes=[bBM])
        k.dma("sp", m.NG[:, :, :], ngT[:, l], writes=[bNG])
        k.op("act", lambda e: e.activation(out=m.SIL[:, :, :], in_=m.CT[:, :, :], func=AF.Silu), reads=[bCT], writes=[bSIL])
        wc = m.wcols
        nblk = 9 * D // wc
        nq = wc // 128
        wsrc = w_mod[l].rearrange("(kc p) n -> p kc n", p=128)

        def load(i):
            k.dma("pool", m.WM[i % nwm][:, :, :], wsrc[:, :, i * wc:(i + 1) * wc], writes=[bWM[i % nwm]])
        for i in range(nwm - 1):
            load(i)
        for j in range(nblk):
            if j + nwm - 1 < nblk:
                load(j + nwm - 1)
            W, bW = m.WM[j % nwm], bWM[j % nwm]
            b = k.banks(1)[0]

            def mm(e):
                ins = None
                for kc in range(KC):
                    ins = e.matmul(ps[0:2, b, 0:wc], lhsT=m.SIL[:, kc, :], rhs=W[:, kc, :], start=(kc == 0), stop=(kc == KC - 1))
                return ins
            k.op("pe", mm, reads=[bSIL, bW], writes=[bps[b]])
            M, bM = m.MR[j % 2], bMR[j % 2]
            k.op("act", lambda e: e.copy(out=M[:, 0:wc], in_=ps[0:2, b, 0:wc]), reads=[bps[b]], writes=[bM])
            b2 = k.banks(1)[0]

            def tr(e):
                ins = None
                for q in range(nq):
                    ins = e.matmul(ps[:, b2, q * 2:(q + 1) * 2], lhsT=M[:, q * 128:(q + 1) * 128], rhs=IDN[0:2, 0:2], start=True, stop=True)
                return ins
            k.op("pe", tr, reads=[bM, bIDN], writes=[bps[b2]])
            k.op("dve", lambda e: e.tensor_copy(out=m.MODT[:, j * nq:(j + 1) * nq, :].rearrange("p a b -> p (a b)"), in_=ps[:, b2, 0:2 * nq]),
                 reads=[bps[b2]], writes=[bMODT])
            yield
        for s_ in range(2):
            k.op("dve", lambda e: e.tensor_tensor(out=m.MODT[:, :, s_], in0=m.MODT[:, :, s_], in1=m.BM[:, :], op=ALU.add), reads=[bMODT, bBM], writes=[bMODT])
        for u in range(3):
            for s_ in range(2):
                sh = m.MODT[:, (3 * u) * 32:(3 * u + 1) * 32, s_]
                sc = m.MODT[:, (3 * u + 1) * 32:(3 * u + 2) * 32, s_]
                gt = m.MODT[:, (3 * u + 2) * 32:(3 * u + 3) * 32, s_]
                A, B, C = abc(l, u, 0)[:, :, s_], abc(l, u, 1)[:, :, s_], abc(l, u, 2)[:, :, s_]
                k.op("dve", lambda e: e.scalar_tensor_tensor(out=A, in0=sc, scalar=1.0, in1=m.NG[:, 2 * u, :], op0=ALU.add, op1=ALU.mult), reads=[bMODT, bNG], writes=[bABC])
                k.op("dve", lambda e: e.tensor_copy(out=B, in_=sh), reads=[bMODT], writes=[bABC])
                mult = 1.0 if u == 1 else 0.5
                k.op("dve", lambda e: e.scalar_tensor_tensor(out=C, in0=gt, scalar=mult, in1=m.NG[:, 2 * u + 1, :], op0=ALU.mult, op1=ALU.mult), reads=[bMODT, bNG], writes=[bABC])
        yield

    def p_mod0():
        LBL = k.sb([128, 2, 16], F32)
        bLBL = Buf()
        k.dma("sp", LBL[:, :, :], lblT, writes=[bLBL])
        k.op("dve", lambda e: e.memset(LB[:, 0, :], 0.0), writes=[bSM])
        k.op("dve", lambda e: e.tensor_tensor(out=LB[:, 1, :], in0=LBL[:, 1, :], in1=LBL[:, 0, :], op=ALU.subtract), reads=[bLBL], writes=[bSM])
        k.op("act", lambda e: e.activation(out=LB[:, 1, :], in_=LB[:, 1, :], func=AF.Exp, scale=-1.0), reads=[bSM], writes=[bSM])
        k.op("dve", lambda e: e.tensor_scalar(out=LB[:, 1, :], in0=LB[:, 1, :], scalar1=1.0, scalar2=None, op0=ALU.add), reads=[bSM], writes=[bSM])
        k.op("dve", lambda e: e.reciprocal(out=LB[:, 1, :], in_=LB[:, 1, :]), reads=[bSM], writes=[bSM])
        k.op("dve", lambda e: e.tensor_scalar(out=OMLB[:, :, :], in0=LB[:, :, :], scalar1=-1.0, scalar2=1.0, op0=ALU.mult, op1=ALU.add), reads=[bSM], writes=[bSM])
        m = mod_alloc(512, 3)
        for _ in mod_gen(0, m):
            pass
        phase_end()

    BG = {"gen": None}

    def bg_alloc():
        if BG["gen"] is None:
            return
        m = mod_alloc(256, 2)
        if BG["gen"] == "start":
            BG["gen"] = mod_gen(1, m)

    def bg_step(n=1):
        g_ = BG["gen"]
        if g_ is None or g_ == "start":
            return
        for _ in range(n):
            try:
                next(g_)
            except StopIteration:
                BG["gen"] = None
                return

    def bg_drain():
        while BG["gen"] is not None and BG["gen"] != "start":
            bg_step(8)

    class G:
        pass

    def gemm_setup(nstage=4, nbf=2):
        g = G()
        g.HT = k.sb([128, KC, T], BF16)
        g.bHT = [Buf() for _ in range(KC)]
        g.WB = [k.sb([128, KC, 128], BF16) for _ in range(2)]
        g.bWB = [Buf(), Buf()]
        g.ST = [k.sb([128, 1280], F32) for _ in range(nstage)]
        g.bST = [Buf() for _ in range(nstage)]
        g.SB = [k.sb([128, 1280], BF16) for _ in range(nbf)]
        g.bSB = [Buf() for _ in range(nbf)]
        g.si = 0
        g.bi = 0
        return g

    def stg(g):
        i = g.si % len(g.ST)
        g.si += 1
        return g.ST[i], g.bST[i]

    def stgb(g):
        i = g.bi % len(g.SB)
        g.bi += 1
        return g.SB[i], g.bSB[i]

    def prologue_norm(g, l, u):
        for c in range(KC):
            for hi, (c0, ncol, tiles) in enumerate(HALVES):
                S, bS = stg(g)
                k.dma("sp", S[:, 0:ncol], rows(sT, c)[:, c0:c0 + ncol], writes=[bS])
                k.op("dve", lambda e: e.tensor_tensor(out=S[:, 0:ncol], in0=S[:, 0:ncol], in1=RS[:, c0:c0 + ncol], op=ALU.mult), reads=[bS, bRS], writes=[bS])
                for (seg, lo, hi2) in HSEG[hi]:
                    k.op("act", lambda e: e.activation(out=g.HT[:, c, lo:hi2], in_=S[:, lo - c0:hi2 - c0], func=AF.Identity,
                                                       scale=abc(l, u, 0)[:, c, seg:seg + 1], bias=abc(l, u, 1)[:, c, seg:seg + 1]),
                         reads=[bS, bABC], writes=[g.bHT[c]])

    def prologue_load(g, src, nchunks=KC):
        for c in range(nchunks):
            k.dma("sp", g.HT[:, c, :], rows(src, c), writes=[g.bHT[c]])

    def gemm(g, chunks, wsrc, kc0_of, nkc, epi, latonly_of=None):
        n = len(chunks)

        def load(i):
            k.dma("pool", g.WB[i % 2][:, 0:nkc, :], wsrc(chunks[i]), writes=[g.bWB[i % 2]])
        load(0)
        for i, desc in enumerate(chunks):
            if i + 1 < n:
                load(i + 1)
            w, bw = g.WB[i % 2], g.bWB[i % 2]
            kc0 = kc0_of(desc)
            if latonly_of is not None:
                set_tokens(latonly_of(desc))
            for hi, (c0, ncol, tiles) in enumerate(list(HALVES)):
                banks = k.banks(len(tiles))

                def mm(e):
                    ins = None
                    for kc in range(nkc):
                        for bi, (t0, tn) in zip(banks, tiles):
                            ins = e.matmul(ps[:, bi, 0:tn], lhsT=w[:, kc, :], rhs=g.HT[:, kc0 + kc, t0:t0 + tn],
                                           start=(kc == 0), stop=(kc == nkc - 1))
                    return ins
                k.op("pe", mm, reads=[bw] + g.bHT[kc0:kc0 + nkc], writes=[bps[b] for b in banks])
                epi(desc, hi, banks)

    def wview(w2d, col0, nkc=KC, row0=0):
        return w2d[row0:row0 + nkc * 128, col0:col0 + 128].rearrange("(kc p) n -> p kc n", p=128)

    def epi_store(g, dst, func=None):
        def f(desc, hi, banks):
            j = desc[1]
            c0, ncol, _ = HALVES[hi]
            S, bS = stg(g)
            src = k.pflat(banks, ncol)
            if func is None:
                k.op("act", lambda e: e.copy(out=S[:, 0:ncol], in_=src), reads=[bps[b] for b in banks], writes=[bS])
            else:
                k.op("act", lambda e: e.activation(out=S[:, 0:ncol], in_=src, func=func), reads=[bps[b] for b in banks], writes=[bS])
            k.dma("sp", rows(dst, j)[:, c0:c0 + ncol], S[:, 0:ncol], reads=[bS])
        return f

    def epi_pair(g, dst, func, hold):
        def f(desc, hi, banks):
            kind, j = desc[0], desc[1]
            c0, ncol, _ = HALVES[hi]
            src = k.pflat(banks, ncol)
            if kind == "g":
                S, bS = stg(g)
                hold[hi] = (S, bS)
                k.op("act", lambda e: e.activation(out=S[:, 0:ncol], in_=src, func=func), reads=[bps[b] for b in banks], writes=[bS])
            else:
                S, bS = hold[hi]
                if dst.dtype == BF16:
                    U, bU = stgb(g)
                else:
                    U, bU = stg(g)
                k.op("dve", lambda e: e.tensor_tensor(out=U[:, 0:ncol], in0=S[:, 0:ncol], in1=src, op=ALU.mult),
                     reads=[bS] + [bps[b] for b in banks], writes=[bU])
                k.dma("sp", rows(dst, j)[:, c0:c0 + ncol], U[:, 0:ncol], reads=[bU])
        return f

    def epi_y(g):
        def f(desc, hi, banks):
            j = desc[1]
            c0, ncol, _ = HALVES[hi]
            src = k.pflat(banks, ncol)
            S, bS = stg(g)
            k.op("act", lambda e: e.copy(out=S[:, 0:ncol], in_=src), reads=[bps[b] for b in banks], writes=[bS])
            k.dma("sp", rows(yT, j)[:, c0:c0 + ncol], S[:, 0:ncol], reads=[bS])
            if desc[2]:
                k.op("act", lambda e: e.activation(out=RS[:, c0:c0 + ncol], in_=src, func=AF.Square), reads=[bps[b] for b in banks], writes=[bRS])
            else:
                Q, bQ = stg(g)
                k.op("act", lambda e: e.activation(out=Q[:, 0:ncol], in_=src, func=AF.Square), reads=[bps[b] for b in banks], writes=[bQ])
                k.op("dve", lambda e: e.tensor_tensor(out=RS[:, c0:c0 + ncol], in0=RS[:, c0:c0 + ncol], in1=Q[:, 0:ncol], op=ALU.add), reads=[bQ, bRS], writes=[bRS])
        return f

    def rs_finalize(scale):
        for hi, (c0, ncol, tiles) in enumerate(HALVES):
            ones_acc(RS[:, c0:c0 + ncol], bRS, c0, True, True)
        rs_from_psum([0, 1, 2, 3, 4], scale)

    def p_residual(l, u):
        SS = [k.sb([128, 1280], F32) for _ in range(3)]
        YS = [k.sb([128, 1280], F32) for _ in range(3)]
        QS = [k.sb([128, 1280], F32) for _ in range(2)]
        bSS = [Buf() for _ in range(3)]
        bYS = [Buf() for _ in range(3)]
        bQS = [Buf() for _ in range(2)]
        i = 0
        for c in range(KC):
            for hi, (c0, ncol, tiles) in enumerate(HALVES):
                S, bS, Y, bY, Q, bQ = SS[i % 3], bSS[i % 3], YS[i % 3], bYS[i % 3], QS[i % 2], bQS[i % 2]
                i += 1
                k.dma("sp", S[:, 0:ncol], rows(sT, c)[:, c0:c0 + ncol], writes=[bS])
                k.dma("sp", Y[:, 0:ncol], rows(yT, c)[:, c0:c0 + ncol], writes=[bY])
                k.op("dve", lambda e: e.tensor_tensor(out=Y[:, 0:ncol], in0=Y[:, 0:ncol], in1=RS[:, c0:c0 + ncol], op=ALU.mult), reads=[bY, bRS], writes=[bY])
                for (seg, lo, hi2) in HSEG[hi]:
                    k.op("dve", lambda e: e.scalar_tensor_tensor(out=S[:, lo - c0:hi2 - c0], in0=Y[:, lo - c0:hi2 - c0], scalar=abc(l, u, 2)[:, c, seg:seg + 1],
                                                                 in1=S[:, lo - c0:hi2 - c0], op0=ALU.mult, op1=ALU.add), reads=[bY, bS, bABC], writes=[bS])
                k.dma("pool", rows(sT, c)[:, c0:c0 + ncol], S[:, 0:ncol], reads=[bS])
                k.op("act", lambda e: e.activation(out=Q[:, 0:ncol], in_=S[:, 0:ncol], func=AF.Square), reads=[bS], writes=[bQ])
                ones_acc(Q, bQ, c0, c == 0, c == KC - 1)
        rs_from_psum([0, 1, 2, 3, 4], 1.0 / D)
        phase_end()

    def p_ffn(l, u, wi):
        g = gemm_setup()
        prologue_norm(g, l, u)
        w1 = ffn_w1[l, wi]
        chunks = []
        for j in range(KC):
            chunks += [("g", j), ("v", j)]
        gemm(g, chunks, lambda d: wview(w1, (0 if d[0] == "g" else D) + d[1] * 128), lambda d: 0, KC,
             epi_pair(g, uT, AF.Silu, {}))
        phase_end()
        g = gemm_setup()
        prologue_load(g, uT)
        w2 = ffn_w2[l, wi]
        gemm(g, [("y", j, j == 0) for j in range(KC)], lambda d: wview(w2, d[1] * 128), lambda d: 0, KC, epi_y(g))
        rs_finalize(1.0 / D)
        phase_end()
        p_residual(l, u)

    def p_mix_a(l):
        g = gemm_setup(4, 0)
        prologue_norm(g, l, 1)
        W = w_in[l]
        TK = [k.sb([128, 4, 128], BF16) for _ in range(2)]
        bTK = [Buf(), Buf()]
        tki = [0]

        def epi_tok(dst, ncolsdst):
            dv = dst.rearrange("(tt p) f -> p tt f", p=128)

            def f(desc, hi, banks):
                j = desc[1]
                c0, ncol, _ = HALVES[hi]
                S, bS = stg(g)
                src = k.pflat(banks, ncol)
                k.op("act", lambda e: e.copy(out=S[:, 0:ncol], in_=src), reads=[bps[b] for b in banks], writes=[bS])
                ntt = ncol // 128
                for t4 in range(0, ntt, 4):
                    nq = min(4, ntt - t4)
                    b = k.banks(1)[0]

                    def tr(e):
                        ins = None
                        for q in range(nq):
                            ins = e.transpose(ps[:, b, q * 128:(q + 1) * 128], S[:, (t4 + q) * 128:(t4 + q + 1) * 128], IDN[:, :])
                        return ins
                    k.op("pe", tr, reads=[bS, bIDN], writes=[bps[b]])
                    Tk, bT = TK[tki[0] % 2], bTK[tki[0] % 2]
                    tki[0] += 1
                    k.op("dve", lambda e: e.tensor_copy(out=Tk[:, 0:nq, :].rearrange("p a b -> p (a b)"), in_=ps[:, b, 0:nq * 128]), reads=[bps[b]], writes=[bT])
                    tt0 = c0 // 128 + t4
                    k.dma("sp", dv[:, tt0:tt0 + nq, j * 128:(j + 1) * 128], Tk[:, 0:nq, :], reads=[bT])
            return f

        e_F = epi_store(g, zF)
        e_HQ = epi_store(g, zHQ)
        e_AQ = epi_store(g, zAQ)
        e_K = epi_store(g, zK)
        e_SC = epi_store(g, zSC)
        e_HG = epi_store(g, zHG, AF.Sigmoid)
        e_GT = epi_store(g, zGATE, AF.Sigmoid)
        e_GLU = epi_pair(g, zGLU, AF.Sigmoid, {})
        e_I = epi_tok(vI, 1024)
        e_V = epi_tok(vV, 256)
        chunks = []
        for j in range(16):
            chunks.append(("F", j, j, e_F))
        for j in range(8):
            chunks.append(("I", j, 16 + j, e_I))
        for j in range(2):
            chunks.append(("K", j, 24 + j, e_K))
        for j in range(2):
            chunks.append(("V", j, 26 + j, e_V))
        for j in range(8):
            chunks.append(("HQ", j, 28 + j, e_HQ))
        for j in range(8):
            chunks.append(("HG", j, 36 + j, e_HG))
        for j in range(8):
            chunks.append(("AQ", j, 44 + j, e_AQ))
        for j in range(24):
            chunks.append(("SC", j, 52 + j, e_SC))
        for j in range(8):
            chunks.append(("g", j, 84 + j, e_GLU))
            chunks.append(("v", j, 76 + j, e_GLU))
        for j in range(128):
            chunks.append(("GT", j, 92 + j, e_GT))
        gemm(g, chunks, lambda d: wview(W, d[2] * 128), lambda d: 0, KC, lambda d, hi, banks: d[3](d, hi, banks),
             latonly_of=(lambda d: d[2] >= 28) if l == 1 else None)
        set_tokens(False)
        phase_end()

    def p_hgrn(l):
        CHM = k.sb([128, T], F32)
        TRI = k.sb([32, 2, 512], F32)
        bC = Buf()
        k.dma("sp", CHM[:, :], chm_in, writes=[bC])
        k.dma("sp", TRI[:, :, :], tri_in, writes=[bC])
        T1, T2, T3, T4, O = [k.sb([128, T], F32) for _ in range(5)]
        T5 = RS
        b1, b2, b3, b4, b5, bO = [Buf() for _ in range(6)]
        QT = [k.sb([128, T], BF16) for _ in range(2)]
        KD = [k.sb([128, T], BF16) for _ in range(2)]
        KOT = [k.sb([32, 72, 128], BF16) for _ in range(2)]
        VT = [k.sb([32, 72, 128], BF16) for _ in range(2)]
        DEC = [k.sb([128, 72], F32) for _ in range(2)]
        bQT, bKD, bKOT, bVT, bDEC = [[Buf(), Buf()] for _ in range(5)]
        SALL = k.sb([128, 72, 128], BF16)
        bSALL = Buf()
        S = [k.sb([128, 128], F32) for _ in range(2)]
        bS = [Buf(), Buf()]
        SCM = [k.sb([32, 512], BF16) for _ in range(2)]
        bSCM = [Buf(), Buf()]
        R1 = k.sb([128, 1280], F32)
        R2 = k.sb([128, 1280], F32)
        bR1, bR2 = Buf(), Buf()
        YB = k.sb([128, T], BF16)
        bYB = Buf()

        def v3(ap):
            return ap[:, :].rearrange("p (c i) -> p c i", i=32)

        def stageA(h, d, p):
            if d == 0:
                k.dma("sp", VT[h % 2][:, :, :], vI[:, h * 128:(h + 1) * 128].rearrange("(c i) d -> i c d", i=32), writes=[bVT[h % 2]])
            lb = LB[:, l, d * 8 + h:d * 8 + h + 1]
            omlb = OMLB[:, l, d * 8 + h:d * 8 + h + 1]
            k.dma("sp", T1[:, :], rows(zF, d * 8 + h), writes=[b1])
            yield
            k.op("act", lambda e: e.activation(out=T1[:, :], in_=T1[:, :], func=AF.Exp, scale=-1.0), reads=[b1], writes=[b1])
            yield
            k.op("dve", lambda e: e.tensor_scalar(out=T1[:, :], in0=T1[:, :], scalar1=1.0, scalar2=None, op0=ALU.add), reads=[b1], writes=[b1])
            yield
            k.op("dve", lambda e: e.reciprocal(out=T1[:, :], in_=T1[:, :]), reads=[b1], writes=[b1])
            yield
            if l > 0:
                k.op("dve", lambda e: e.tensor_scalar(out=T1[:, :], in0=T1[:, :], scalar1=omlb, scalar2=lb, op0=ALU.mult, op1=ALU.add), reads=[b1, bSM], writes=[b1])
                yield
            k.op("act", lambda e: e.activation(out=T2[:, :], in_=T1[:, :], func=AF.Ln), reads=[b1], writes=[b2])
            yield
            k.op("dve", lambda e: e.tensor_scalar(out=T3[:, :], in0=T1[:, :], scalar1=-1.0, scalar2=1.0, op0=ALU.mult, op1=ALU.add), reads=[b1], writes=[b3])
            yield
            k.op("dve", lambda e: e.tensor_tensor_scan(out=T4[:, :], data0=CHM[:, :], data1=T2[:, :], initial=0.0, op0=ALU.mult, op1=ALU.add), reads=[bC, b2], writes=[b4])
            yield
            totb = v3(T4)[:, :, 31:32].broadcast_to([128, 72, 32])
            if d == 0:
                C, bCc = T4, b4
            else:
                k.op("dve", lambda e: e.tensor_tensor(out=v3(T1), in0=totb, in1=v3(T4), op=ALU.subtract), reads=[b4], writes=[b1])
                yield
                k.op("dve", lambda e: e.tensor_tensor(out=T1[:, :], in0=T1[:, :], in1=T2[:, :], op=ALU.add), reads=[b1, b2], writes=[b1])
                yield
                C, bCc = T1, b1
            k.dma("sp", T2[:, :], rows(zHQ, h), reads=[], writes=[b2])
            k.op("act", lambda e: e.activation(out=T5[:, :], in_=C[:, :], func=AF.Exp), reads=[bCc], writes=[b5])
            yield
            k.op("dve", lambda e: e.tensor_tensor(out=QT[p][:, :], in0=T2[:, :], in1=T5[:, :], op=ALU.mult), reads=[b2, b5], writes=[bQT[p]])
            yield
            k.op("act", lambda e: e.activation(out=T5[:, :], in_=C[:, :], func=AF.Exp, scale=-1.0), reads=[bCc], writes=[b5])
            yield
            k.op("dve", lambda e: e.tensor_tensor(out=KD[p][:, :], in0=T3[:, :], in1=T5[:, :], op=ALU.mult), reads=[b3, b5], writes=[bKD[p]])
            yield
            k.op("dve", lambda e: e.tensor_tensor(out=v3(T5), in0=totb, in1=v3(C), op=ALU.subtract), reads=[b4, bCc], writes=[b5])
            yield
            k.op("act", lambda e: e.activation(out=T5[:, :], in_=T5[:, :], func=AF.Exp), reads=[b5], writes=[b5])
            yield
            k.op("dve", lambda e: e.tensor_tensor(out=T5[:, :], in0=T3[:, :], in1=T5[:, :], op=ALU.mult), reads=[b3, b5], writes=[b5])
            yield
            k.op("act", lambda e: e.activation(out=DEC[p][:, :], in_=v3(T4)[:, :, 31], func=AF.Exp), reads=[b4], writes=[bDEC[p]])
            yield
            for c4 in range(18):
                b = k.banks(1)[0]

                def tr(e):
                    ins = None
                    for q in range(4):
                        c = c4 * 4 + q
                        ins = e.transpose(ps[0:32, b, q * 128:(q + 1) * 128], T5[:, c * 32:(c + 1) * 32], IDN[:, :])
                    return ins
                k.op("pe", tr, reads=[b5, bIDN], writes=[bps[b]])
                k.op("act", lambda e: e.copy(out=KOT[p][:, c4 * 4:(c4 + 1) * 4, :].rearrange("p a b -> p (a b)"), in_=ps[0:32, b, :]), reads=[bps[b]], writes=[bKOT[p]])
                yield

        def stageB(h, d, p):
            V, bV = VT[h % 2], bVT[h % 2]
            order = (list(range(64, 72)) + list(range(64))) if d == 0 else (list(range(71, 63, -1)) + list(range(63, -1, -1)))
            k.op("dve", lambda e: e.memset(S[0][:, :], 0.0), writes=[bS[0]])
            for n_, c in enumerate(order):
                src, bsrc, dst, bdst = S[n_ % 2], bS[n_ % 2], S[(n_ + 1) % 2], bS[(n_ + 1) % 2]
                k.op("act", lambda e: e.copy(out=SALL[:, c, :], in_=src[:, :]), reads=[bsrc], writes=[bSALL])
                b = k.banks(1)[0]
                k.op("pe", lambda e: e.matmul(ps[:, b, 0:128], lhsT=KOT[p][:, c, :], rhs=V[:, c, :], start=True, stop=True), reads=[bKOT[p], bV], writes=[bps[b]])
                k.op("dve", lambda e: e.scalar_tensor_tensor(out=dst[:, :], in0=src[:, :], scalar=DEC[p][:, c:c + 1], in1=ps[:, b, 0:128], op0=ALU.mult, op1=ALU.add),
                     reads=[bsrc, bDEC[p], bps[b]], writes=[bdst])
                yield
            for gi, g0 in enumerate(range(0, 72, 16)):
                ng = min(16, 72 - g0)
                b = k.banks(1)[0]

                def sc(e):
                    ins = None
                    for ci in range(ng):
                        c = g0 + ci
                        ins = e.matmul(ps[0:32, b, ci * 32:(ci + 1) * 32], lhsT=KD[p][:, c * 32:(c + 1) * 32], rhs=QT[p][:, c * 32:(c + 1) * 32], start=True, stop=True)
                    return ins
                k.op("pe", sc, reads=[bKD[p], bQT[p]], writes=[bps[b]])
                M, bM = SCM[gi % 2], bSCM[gi % 2]
                k.op("dve", lambda e: e.tensor_tensor(out=M[:, 0:ng * 32], in0=ps[0:32, b, 0:ng * 32], in1=TRI[:, d, 0:ng * 32], op=ALU.mult), reads=[bps[b], bC], writes=[bM])
                b2_ = k.banks(1)[0]

                def om(e):
                    ins = None
                    for ci in range(ng):
                        c = g0 + ci
                        e.matmul(ps[:, b2_, ci * 32:(ci + 1) * 32], lhsT=V[:, c, :], rhs=M[:, ci * 32:(ci + 1) * 32], start=True, stop=False)
                        ins = e.matmul(ps[:, b2_, ci * 32:(ci + 1) * 32], lhsT=SALL[:, c, :], rhs=QT[p][:, c * 32:(c + 1) * 32], start=False, stop=True)
                    return ins
                k.op("pe", om, reads=[bV, bM, bSALL, bQT[p]], writes=[bps[b2_]])
                if d == 0:
                    k.op("act", lambda e: e.copy(out=O[:, g0 * 32:(g0 + ng) * 32], in_=ps[:, b2_, 0:ng * 32]), reads=[bps[b2_]], writes=[bO])
                else:
                    k.op("dve", lambda e: e.tensor_tensor(out=O[:, g0 * 32:(g0 + ng) * 32], in0=O[:, g0 * 32:(g0 + ng) * 32], in1=ps[:, b2_, 0:ng * 32], op=ALU.add), reads=[bps[b2_], bO], writes=[bO])
                yield
            if d == 1:
                if "oDbg" in dbg:
                    k.dma("sp", rows(oDbg, h), O[:, :], reads=[bO])
                for hi, (c0, ncol, tiles) in enumerate(HALVES):
                    k.dma("sp", R2[:, 0:ncol], rows(zHG, h)[:, c0:c0 + ncol], writes=[bR2])
                    k.op("act", lambda e: e.activation(out=R1[:, 0:ncol], in_=O[:, c0:c0 + ncol], func=AF.Square), reads=[bO], writes=[bR1])
                    ones_acc(R1, bR1, c0, True, True)
                    bk = [t0 // 512 for (t0, tn) in tiles]
                    k.op("act", lambda e: e.activation(out=R1[:, 0:ncol], in_=k.pflat(bk, ncol), func=AF.Sqrt, scale=1.0 / 128, bias=EPS), reads=[bps[i] for i in bk], writes=[bR1])
                    k.op("dve", lambda e: e.reciprocal(out=R1[:, 0:ncol], in_=R1[:, 0:ncol]), reads=[bR1], writes=[bR1])
                    k.op("dve", lambda e: e.tensor_tensor(out=R1[:, 0:ncol], in0=R1[:, 0:ncol], in1=O[:, c0:c0 + ncol], op=ALU.mult), reads=[bR1, bO], writes=[bR1])
                    k.op("dve", lambda e: e.scalar_tensor_tensor(out=YB[:, c0:c0 + ncol], in0=R1[:, 0:ncol], scalar=HGN[:, l:l + 1], in1=R2[:, 0:ncol], op0=ALU.mult, op1=ALU.mult), reads=[bR1, bR2, bSM], writes=[bYB])
                    yield
                k.dma("pool", rows(brT, h), YB[:, :], reads=[bYB])

        its = [(h, d) for h in range(8) for d in range(2)]
        for _ in stageA(0, 0, 0):
            pass
        for i, (h, d) in enumerate(its):
            gb = stageB(h, d, i % 2)
            ga = stageA(its[i + 1][0], its[i + 1][1], (i + 1) % 2) if i + 1 < len(its) else None
            alive_a, alive_b = ga is not None, True
            while alive_a or alive_b:
                if alive_b:
                    try:
                        next(gb)
                    except StopIteration:
                        alive_b = False
                if alive_a:
                    try:
                        next(ga)
                    except StopIteration:
                        alive_a = False
        phase_end()

    def p_attn(l):
        if l == 0:
            BG["gen"] = "start"
        bg_alloc()
        ROT = k.sb([128, 128], F32)
        COS = k.sb([128, TL], F32)
        SIN = k.sb([128, TL], F32)
        bR = Buf()
        k.dma("sp", ROT[:, :], rot_in, writes=[bR])
        k.dma("sp", COS[:, :], cos_in, writes=[bR])
        k.dma("sp", SIN[:, :], sin_in, writes=[bR])
        X, XN, R, TM = [k.sb([128, T], F32) for _ in range(4)]
        bX, bXN, bRr, bTM = [Buf() for _ in range(4)]
        KT = k.sb([128, T], BF16)
        QT = k.sb([128, T], BF16)
        VT = k.sb([128, 18, 128], BF16)
        bKT, bQT, bVT = Buf(), Buf(), Buf()
        PT = [k.sb([128, 512], BF16) for _ in range(4)]
        bPT = [Buf() for _ in range(4)]
        RD = k.sb([128, 512], F32)
        AO = [k.sb([128, 512], BF16) for _ in range(2)]
        bRD = Buf()
        bAO = [Buf(), Buf()]

        def qk_prep(src, gcol, DST, bDST):
            k.dma("sp", X[:, :], src, writes=[bX])
            k.op("act", lambda e: e.activation(out=TM[:, :], in_=X[:, :], func=AF.Square), reads=[bX], writes=[bTM])
            for hi, (c0, ncol, tiles) in enumerate(HALVES):
                ones_acc(TM[:, c0:c0 + ncol], bTM, c0, True, True)
            k.op("act", lambda e: e.activation(out=R[:, :], in_=k.pflat([0, 1, 2, 3, 4], T), func=AF.Sqrt, scale=1.0 / 128, bias=EPS), reads=[bps[i] for i in range(5)], writes=[bRr])
            k.op("dve", lambda e: e.reciprocal(out=R[:, :], in_=R[:, :]), reads=[bRr], writes=[bRr])
            k.op("dve", lambda e: e.scalar_tensor_tensor(out=XN[:, :], in0=X[:, :], scalar=QKN[:, gcol:gcol + 1], in1=R[:, :], op0=ALU.mult, op1=ALU.mult), reads=[bX, bRr, bSM], writes=[bXN])

            def mm(e):
                ins = None
                for t in range(4):
                    ins = e.matmul(ps[:, t, :], lhsT=ROT[:, :], rhs=XN[:, t * 512:(t + 1) * 512], start=True, stop=True)
                return ins
            k.op("pe", mm, reads=[bXN, bR], writes=[bps[i] for i in range(4)])
            k.op("dve", lambda e: e.tensor_tensor(out=TM[:, 0:TL], in0=k.pflat([0, 1, 2, 3], TL), in1=SIN[:, :], op=ALU.mult), reads=[bps[i] for i in range(4)] + [bR], writes=[bTM])
            k.op("dve", lambda e: e.tensor_tensor(out=R[:, 0:TL], in0=XN[:, 0:TL], in1=COS[:, :], op=ALU.mult), reads=[bXN, bR, bRr], writes=[bRr])
            k.op("dve", lambda e: e.tensor_tensor(out=DST[:, 0:TL], in0=R[:, 0:TL], in1=TM[:, 0:TL], op=ALU.add), reads=[bRr, bTM], writes=[bDST])
            k.op("act", lambda e: e.copy(out=DST[:, TL:T], in_=XN[:, TL:T]), reads=[bXN], writes=[bDST])

        it = 0
        pi = 0
        for gk in range(2):
            qk_prep(rows(zK, gk), 2 * l + 1, KT, bKT)
            k.dma("sp", VT[:, :, :], vV[:, gk * 128:(gk + 1) * 128].rearrange("(tt p) d -> p tt d", p=128), writes=[bVT])
            for hq4 in range(4):
                hq = gk * 4 + hq4
                qk_prep(rows(zAQ, hq), 2 * l, QT, bQT)
                for (q0, qn, kts) in [(0, 512, range(18)), (512, 512, range(18)), (1024, 512, range(18)), (1536, 512, range(18)), (2048, 256, (16, 17))]:
                    bx, by = (0, 1) if it % 2 == 0 else (2, 3)
                    it += 1
                    kts = list(kts)
                    for ki, kt in enumerate(kts):
                        bs = 4 + (pi % 4)
                        P, bP = PT[pi % 4], bPT[pi % 4]
                        pi += 1
                        k.op("pe", lambda e: e.matmul(ps[:, bs, 0:qn], lhsT=KT[:, kt * 128:(kt + 1) * 128], rhs=QT[:, q0:q0 + qn], start=True, stop=True), reads=[bKT, bQT], writes=[bps[bs]])
                        k.op("act", lambda e: e.activation(out=P[:, 0:qn], in_=ps[:, bs, 0:qn], func=AF.Exp, scale=128.0 ** -0.5), reads=[bps[bs]], writes=[bP])

                        def ov(e):
                            e.matmul(ps[:, bx, 0:qn], lhsT=VT[:, kt, :], rhs=P[:, 0:qn], start=(ki == 0), stop=(ki == len(kts) - 1))
                            return e.matmul(ps[:, by, 0:qn], lhsT=ONES_B[:, :], rhs=P[:, 0:qn], start=(ki == 0), stop=(ki == len(kts) - 1))
                        k.op("pe", ov, reads=[bVT, bP, bONES], writes=[bps[bx], bps[by]])
                    k.op("dve", lambda e: e.reciprocal(out=RD[:, 0:qn], in_=ps[:, by, 0:qn]), reads=[bps[by]], writes=[bRD])
                    A, bA = AO[it % 2], bAO[it % 2]
                    k.op("dve", lambda e: e.tensor_tensor(out=A[:, 0:qn], in0=ps[:, bx, 0:qn], in1=RD[:, 0:qn], op=ALU.mult), reads=[bps[bx], bRD], writes=[bA])
                    k.dma("pool", rows(brT, 8 + hq)[:, q0:q0 + qn], A[:, 0:qn], reads=[bA])
                    bg_step(2)
        phase_end()

    def p_sconv(l):
        bg_alloc()
        SCW = k.sb([128, 8, 3], F32)
        bW = Buf()
        k.dma("sp", SCW[:, :, :], scwT[:, l], writes=[bW])
        BG, CG, U_, OUT = [[k.sb([128, T], F32) for _ in range(2)] for _ in range(4)]
        bBG, bCG, bU, bOUT = [[Buf(), Buf()] for _ in range(4)]
        YB = [k.sb([128, T], BF16) for _ in range(2)]
        bYB = [Buf(), Buf()]
        for cc in range(8):
            i = cc % 2
            k.dma("sp", BG[i][:, :], rows(zSC, cc), writes=[bBG[i]])
            k.dma("sp", CG[i][:, :], rows(zSC, 8 + cc), writes=[bCG[i]])
            k.dma("sp", U_[i][:, :], rows(zSC, 16 + cc), writes=[bU[i]])
            M, o = CG[i], OUT[i]
            k.op("dve", lambda e: e.tensor_tensor(out=M[:, :], in0=M[:, :], in1=U_[i][:, :], op=ALU.mult), reads=[bCG[i], bU[i]], writes=[bCG[i]])
            k.op("dve", lambda e: e.tensor_scalar(out=o[:, :], in0=M[:, :], scalar1=SCW[:, cc, 1:2], scalar2=None, op0=ALU.mult), reads=[bCG[i], bW], writes=[bOUT[i]])
            for (lo, hi2) in [(0, TL), (TL, T)]:
                k.op("dve", lambda e: e.scalar_tensor_tensor(out=o[:, lo + 1:hi2], in0=M[:, lo:hi2 - 1], scalar=SCW[:, cc, 0:1], in1=o[:, lo + 1:hi2], op0=ALU.mult, op1=ALU.add), reads=[bCG[i], bW, bOUT[i]], writes=[bOUT[i]])
                k.op("dve", lambda e: e.scalar_tensor_tensor(out=o[:, lo:hi2 - 1], in0=M[:, lo + 1:hi2], scalar=SCW[:, cc, 2:3], in1=o[:, lo:hi2 - 1], op0=ALU.mult, op1=ALU.add), reads=[bCG[i], bW, bOUT[i]], writes=[bOUT[i]])
            k.op("dve", lambda e: e.tensor_tensor(out=YB[i][:, :], in0=o[:, :], in1=BG[i][:, :], op=ALU.mult), reads=[bOUT[i], bBG[i]], writes=[bYB[i]])
            k.dma("pool", rows(brT, 16 + cc), YB[i][:, :], reads=[bYB[i]])
            bg_step(2)
        phase_end()

    def p_conf(l):
        bg_alloc()
        CFW = k.sb([128, 8, 31], F32)
        CFB = k.sb([128, 8], F32)
        LNG = k.sb([128, 8], F32)
        LNB = k.sb([128, 8], F32)
        bW = Buf()
        k.dma("sp", CFW[:, :, :], cfwT[:, l], writes=[bW])
        k.dma("sp", CFB[:, :], cfbT[:, l], writes=[bW])
        k.dma("sp", LNG[:, :], lngT[:, l], writes=[bW])
        k.dma("sp", LNB[:, :], lnbT[:, l], writes=[bW])
        PADL = [k.sb([128, TL + 30], F32) for _ in range(2)]
        PADC = [k.sb([128, TC + 30], F32) for _ in range(2)]
        bPAD = [Buf(), Buf()]
        for i in range(2):
            k.op("dve", lambda e: e.memset(PADL[i][:, :], 0.0), writes=[bPAD[i]])
            k.op("dve", lambda e: e.memset(PADC[i][:, :], 0.0), writes=[bPAD[i]])
        U = [k.sb([128, T], F32) for _ in range(8)]
        bU = [Buf() for _ in range(8)]
        MEAN, RSD, TMP = [k.sb([128, T], F32) for _ in range(3)]
        bMEAN, bRSD, bTMP = Buf(), Buf(), Buf()
        YB = [k.sb([128, T], BF16) for _ in range(2)]
        bYB = [Buf(), Buf()]
        for cc in range(8):
            i = cc % 2
            k.dma("sp", PADL[i][:, 15:15 + TL], rows(zGLU, cc)[:, 0:TL], writes=[bPAD[i]])
            k.dma("sp", PADC[i][:, 15:15 + TC], rows(zGLU, cc)[:, TL:T], writes=[bPAD[i]])
            for (P, lo, n) in [(PADL[i], 0, TL), (PADC[i], TL, TC)]:
                k.op("dve", lambda e: e.tensor_scalar(out=U[cc][:, lo:lo + n], in0=P[:, 0:n], scalar1=CFW[:, cc, 0:1], scalar2=CFB[:, cc:cc + 1], op0=ALU.mult, op1=ALU.add), reads=[bPAD[i], bW], writes=[bU[cc]])
                for tp in range(1, 31):
                    k.op("dve", lambda e: e.scalar_tensor_tensor(out=U[cc][:, lo:lo + n], in0=P[:, tp:tp + n], scalar=CFW[:, cc, tp:tp + 1], in1=U[cc][:, lo:lo + n], op0=ALU.mult, op1=ALU.add), reads=[bPAD[i], bW, bU[cc]], writes=[bU[cc]])
                    if tp % 8 == 0:
                        bg_step(1)
        for cc in range(8):
            for hi, (c0, ncol, tiles) in enumerate(HALVES):
                ones_acc(U[cc][:, c0:c0 + ncol], bU[cc], c0, cc == 0, cc == 7)
        k.op("act", lambda e: e.activation(out=MEAN[:, :], in_=k.pflat([0, 1, 2, 3, 4], T), func=AF.Copy, scale=1.0 / 1024), reads=[bps[i] for i in range(5)], writes=[bMEAN])
        for cc in range(8):
            k.op("act", lambda e: e.activation(out=TMP[:, :], in_=U[cc][:, :], func=AF.Square), reads=[bU[cc]], writes=[bTMP])
            for hi, (c0, ncol, tiles) in enumerate(HALVES):
                ones_acc(TMP[:, c0:c0 + ncol], bTMP, c0, cc == 0, cc == 7)
        k.op("act", lambda e: e.activation(out=TMP[:, :], in_=MEAN[:, :], func=AF.Square), reads=[bMEAN], writes=[bTMP])
        k.op("dve", lambda e: e.scalar_tensor_tensor(out=RSD[:, :], in0=k.pflat([0, 1, 2, 3, 4], T), scalar=1.0 / 1024, in1=TMP[:, :], op0=ALU.mult, op1=ALU.subtract), reads=[bps[i] for i in range(5)] + [bTMP], writes=[bRSD])
        k.op("act", lambda e: e.activation(out=RSD[:, :], in_=RSD[:, :], func=AF.Sqrt, bias=EPS), reads=[bRSD], writes=[bRSD])
        k.op("dve", lambda e: e.reciprocal(out=RSD[:, :], in_=RSD[:, :]), reads=[bRSD], writes=[bRSD])
        for cc in range(8):
            i = cc % 2
            k.op("dve", lambda e: e.tensor_tensor(out=U[cc][:, :], in0=U[cc][:, :], in1=MEAN[:, :], op=ALU.subtract), reads=[bU[cc], bMEAN], writes=[bU[cc]])
            k.op("dve", lambda e: e.tensor_tensor(out=U[cc][:, :], in0=U[cc][:, :], in1=RSD[:, :], op=ALU.mult), reads=[bU[cc], bRSD], writes=[bU[cc]])
            k.op("act", lambda e: e.activation(out=YB[i][:, :], in_=U[cc][:, :], func=AF.Silu, scale=LNG[:, cc:cc + 1], bias=LNB[:, cc:cc + 1]), reads=[bU[cc], bW], writes=[bYB[i]])
            k.dma("pool", rows(brT, 24 + cc), YB[i][:, :], reads=[bYB[i]])
        bg_drain()
        phase_end()

    def p_mix_c(l):
        g = gemm_setup(nstage=0, nbf=2)
        prologue_load(g, brT)
        ACM = [k.sb([128, 1280], F32) for _ in range(2)]
        bACM = [Buf(), Buf()]
        GT = [k.sb([128, 1280], F32) for _ in range(2)]
        bGT = [Buf(), Buf()]
        gi = [0]

        def epi(desc, hi, banks):
            _, j, i = desc
            c0, ncol, _ = HALVES[hi]
            src = k.pflat(banks, ncol)
            Gt, bG = GT[gi[0] % 2], bGT[gi[0] % 2]
            gi[0] += 1
            k.dma("sp", Gt[:, 0:ncol], rows(zGATE, i * 32 + j)[:, c0:c0 + ncol], writes=[bG])
            A, bA = ACM[hi], bACM[hi]
            if i == 0:
                k.op("dve", lambda e: e.tensor_tensor(out=A[:, 0:ncol], in0=src, in1=Gt[:, 0:ncol], op=ALU.mult), reads=[bG] + [bps[b] for b in banks], writes=[bA])
            else:
                k.op("dve", lambda e: e.tensor_tensor(out=Gt[:, 0:ncol], in0=src, in1=Gt[:, 0:ncol], op=ALU.mult), reads=[bG] + [bps[b] for b in banks], writes=[bG])
                k.op("dve", lambda e: e.tensor_tensor(out=A[:, 0:ncol], in0=A[:, 0:ncol], in1=Gt[:, 0:ncol], op=ALU.add), reads=[bG, bA], writes=[bA])
            if i == 3:
                U, bU = stgb(g)
                k.op("act", lambda e: e.copy(out=U[:, 0:ncol], in_=A[:, 0:ncol]), reads=[bA], writes=[bU])
                k.dma("pool", rows(accT, j)[:, c0:c0 + ncol], U[:, 0:ncol], reads=[bU])
        chunks = [("m", j, i) for j in range(KC) for i in range(4)]
        gemm(g, chunks, lambda d: wview(w_branch[l, d[2]], d[1] * 128, nkc=8), lambda d: d[2] * 8, 8, epi)
        phase_end()

    def p_mix_d(l):
        g = gemm_setup()
        prologue_load(g, accT)
        wo = w_out[l]
        gemm(g, [("y", j, j == 0) for j in range(KC)], lambda d: wview(wo, d[1] * 128), lambda d: 0, KC, epi_y(g))
        rs_finalize(1.0 / D)
        phase_end()
        p_residual(l, 1)

    def p_transpose_out():
        ST = [k.sb([128, KC, 128], F32) for _ in range(2)]
        bST = [Buf(), Buf()]
        OT = [k.sb([128, D], F32) for _ in range(2)]
        bOT = [Buf(), Buf()]
        sTv = sT.rearrange("(c p) t -> p c t", p=128)
        for tt in range(16):
            S, bS, O, bO = ST[tt % 2], bST[tt % 2], OT[tt % 2], bOT[tt % 2]
            k.dma("sp", S[:, :, :], sTv[:, :, tt * 128:(tt + 1) * 128], writes=[bS])
            for c4 in range(8):
                b = k.banks(1)[0]

                def tr(e):
                    ins = None
                    for q in range(4):
                        ins = e.transpose(ps[:, b, q * 128:(q + 1) * 128], S[:, c4 * 4 + q, :], IDN[:, :])
                    return ins
                k.op("pe", tr, reads=[bS, bIDN], writes=[bps[b]])
                if c4 % 2 == 0:
                    k.op("act", lambda e: e.copy(out=O[:, c4 * 512:(c4 + 1) * 512], in_=ps[:, b, :]), reads=[bps[b]], writes=[bO])
                else:
                    k.op("dve", lambda e: e.tensor_copy(out=O[:, c4 * 512:(c4 + 1) * 512], in_=ps[:, b, :]), reads=[bps[b]], writes=[bO])
            k.dma("pool", out_d[tt * 128:(tt + 1) * 128, :], O[:, :], reads=[bO])
        phase_end()

    phases = [("tin", p_transpose_in), ("stats", p_stats), ("mod", p_mod0)]
    for l in range(2):
        phases += [("ffn0_%d" % l, lambda l=l: p_ffn(l, 0, 0)),
                   ("mixa_%d" % l, lambda l=l: p_mix_a(l)),
                   ("hgrn_%d" % l, lambda l=l: p_hgrn(l)),
                   ("attn_%d" % l, lambda l=l: p_attn(l)),
                   ("sconv_%d" % l, lambda l=l: p_sconv(l)),
                   ("conf_%d" % l, lambda l=l: p_conf(l)),
                   ("mixc_%d" % l, lambda l=l: (set_tokens(l == 1), p_mix_c(l))),
                   ("mixd_%d" % l, lambda l=l: p_mix_d(l)),
                   ("ffn1_%d" % l, lambda l=l: p_ffn(l, 2, 1))]
    for name, fn in phases:
        fn()
        if stop_after == name:
            break
    p_transpose_out()
    k.barrier()
    return nc


def host_consts():
    idn = np.eye(128, dtype=np.float32)
    rot = np.zeros((128, 128), np.float32)
    for base in (0, 64):
        for m in range(32):
            rot[base + m + 32, base + m] = -1.0
            rot[base + m, base + m + 32] = 1.0
    inv = (10000.0 ** (-np.arange(0, 64, 2, dtype=np.float32) / 64)).astype(np.float32)
    t = np.arange(TL)
    ang_r = (t // 64).astype(np.float32)[None, :] * inv[:, None]
    ang_c = (t % 64).astype(np.float32)[None, :] * inv[:, None]
    ang = np.concatenate([ang_r, ang_r, ang_c, ang_c], 0).astype(np.float32)
    chm = np.ones((128, T), np.float32)
    chm[:, ::32] = 0.0
    j = np.arange(32)[:, None]
    i = np.arange(32)[None, :]
    tri = np.stack([np.tile((j <= i).astype(np.float32), (1, 16)), np.tile((j >= i).astype(np.float32), (1, 16))], 1)
    return dict(idn=idn, rot=rot, cosT=np.cos(ang).astype(np.float32), sinT=np.sin(ang).astype(np.float32), chm=chm, tri=np.ascontiguousarray(tri))


def fm(v):
    v = np.asarray(v, np.float32)
    lead = v.shape[:-1]
    a = v.reshape(lead + (v.shape[-1] // 128, 128))
    return np.ascontiguousarray(np.moveaxis(a, -1, 0))


def make_in_maps(inp, cores):
    cs = host_consts()
    shared = dict(cs)
    shared["w_mod"] = np.ascontiguousarray(inp["w_mod"], np.float32)
    shared["bmodT"] = fm(inp["b_mod"])
    shared["ngT"] = fm(inp["norm_g"])
    shared["ffn_w1"] = np.ascontiguousarray(inp["ffn_w1"], np.float32)
    shared["ffn_w2"] = np.ascontiguousarray(inp["ffn_w2"], np.float32)
    shared["w_in"] = np.ascontiguousarray(inp["w_in"], np.float32)
    lbl = np.asarray(inp["hgrn_lb_logits"], np.float32).reshape(2, 2, 8, 128)
    shared["lblT"] = np.ascontiguousarray(np.transpose(lbl, (3, 0, 1, 2)).reshape(128, 2, 16))
    shared["hgnT"] = np.ascontiguousarray(np.asarray(inp["hgrn_norm_g"], np.float32).T)
    shared["qknT"] = np.ascontiguousarray(np.asarray(inp["qk_norm_g"], np.float32).reshape(4, 128).T)
    shared["scwT"] = np.ascontiguousarray(np.transpose(np.asarray(inp["short_conv_w"], np.float32).reshape(2, 3, 8, 128), (3, 0, 2, 1)))
    shared["cfwT"] = np.ascontiguousarray(np.transpose(np.asarray(inp["conf_dw_w"], np.float32).reshape(2, 31, 8, 128), (3, 0, 2, 1)))
    shared["cfbT"] = fm(inp["conf_dw_b"])
    shared["lngT"] = fm(inp["conf_ln_g"])
    shared["lnbT"] = fm(inp["conf_ln_b"])
    shared["w_branch"] = np.ascontiguousarray(inp["w_branch"], np.float32)
    shared["w_out"] = np.ascontiguousarray(inp["w_out"], np.float32)
    cctx = fm(inp["c_ctx"])
    maps = []
    for b in cores:
        m = dict(shared)
        m["x"] = np.ascontiguousarray(inp["x"][b], np.float32)
        m["ctx"] = np.ascontiguousarray(inp["ctx"][b], np.float32)
        m["cT"] = np.ascontiguousarray(np.stack([fm(inp["c"][b]), cctx], -1))
        maps.append(m)
    return maps


def kernel(**inputs):
    nc = build()
    in_maps = make_in_maps(inputs, list(range(8)))
    res = run_bass_kernel_spmd(nc, in_maps, core_ids=list(range(8)))
    return np.stack([r["out"] for r in res.results], 0).astype(np.float32)
```
